# Optimizing a Trainium2 kernel written in Bass

```python
import jax, jax.numpy as jnp
from jax import lax
import numpy as np

D_MODEL = 1024
BATCH = 16
SEQ = 256
DEPTH = 1
DEC_BATCH = 4
DEC_SEQ = 2048
PAST_LEN = 256

GRID_W = 64
LRU_HEADS = 8
LRU_WIDTH = D_MODEL // 2
LRU_HEAD_DIM = LRU_WIDTH // LRU_HEADS
LRU_CONV_W = 4
LRU_CONV_LEFT = 2
LRU_C = 8.0
SGU_GROUPS = 4
SGU_WIDTH = D_MODEL // 2
SGU_GROUP_DIM = SGU_WIDTH // SGU_GROUPS
CHUNK = 128
D_IN = 2 * LRU_WIDTH + 2 * SGU_WIDTH
D_MIX = LRU_WIDTH + SGU_WIDTH
D_FF = 3 * D_MODEL
FFN_CONV_W = 3
FFN_CONV_LEFT = 1
N_MOD = 6
EPS = 1e-6

kernel_name = "hybrid_rglru_sgu_prefix_diffusion_step"


def rms_norm(x, gain):
    xf = x.astype(jnp.float32)
    y = xf * lax.rsqrt(jnp.mean(xf * xf, axis=-1, keepdims=True) + EPS)
    return (y * gain.astype(jnp.float32)).astype(x.dtype)


def dw_conv(x, w, b, left):
    k_w = w.shape[0]
    n = x.shape[-2]
    pad = [(0, 0)] * (x.ndim - 2) + [(left, k_w - 1 - left), (0, 0)]
    xp = jnp.pad(x, pad)
    out = b + xp[..., 0:n, :] * w[0]
    for k in range(1, k_w):
        out = out + xp[..., k:k + n, :] * w[k]
    return out


def _lin_combine(earlier, later):
    a1, b1 = earlier
    a2, b2 = later
    return a1 * a2, a2 * b1 + b2


def rglru_direction(xc, h0, w_a, b_a, w_x, b_x, lam, reverse):
    bsz, n, _ = xc.shape
    xh = xc.reshape(bsz, n, LRU_HEADS, LRU_HEAD_DIM)
    r = jax.nn.sigmoid(jnp.einsum("bshi,hij->bshj", xh, w_a).reshape(bsz, n, LRU_WIDTH) + b_a)
    i = jax.nn.sigmoid(jnp.einsum("bshi,hij->bshj", xh, w_x).reshape(bsz, n, LRU_WIDTH) + b_x)
    log_a = -LRU_C * r * jax.nn.softplus(-lam)
    a = jnp.exp(log_a)
    b = jnp.sqrt(-jnp.expm1(2.0 * log_a)) * (i * xc)
    edge = n - 1 if reverse else 0
    b = b.at[:, edge].add(a[:, edge] * h0.astype(b.dtype))
    _, h = lax.associative_scan(_lin_combine, (a, b), axis=1, reverse=reverse)
    return h


def mixer(h, h0_fwd, h0_bwd, p):
    z = h @ p["w_in"]
    xr, yr, u, v = jnp.split(z, [LRU_WIDTH, 2 * LRU_WIDTH, 2 * LRU_WIDTH + SGU_WIDTH], axis=-1)
    xc = dw_conv(xr, p["lru_conv_w"], p["lru_conv_b"], LRU_CONV_LEFT)
    hf = rglru_direction(xc, h0_fwd, p["lru_wa"][0], p["lru_ba"][0], p["lru_wx"][0],
                         p["lru_bx"][0], p["lru_lam"][0], False)
    hb = rglru_direction(xc, h0_bwd, p["lru_wa"][1], p["lru_ba"][1], p["lru_wx"][1],
                         p["lru_bx"][1], p["lru_lam"][1], True)
    o_lru = jax.nn.gelu(yr) * (hf + hb)
    bsz, n, _ = v.shape
    u = jax.nn.gelu(u)
    vc = jax.nn.gelu(v).reshape(bsz, n // CHUNK, CHUNK, SGU_GROUPS, SGU_GROUP_DIM)
    s = jnp.einsum("gpq,bnqgc->bnpgc", p["sgu_ws"], vc) + p["sgu_bs"].T[:, :, None]
    o_sgu = u * s.reshape(bsz, n, SGU_WIDTH)
    o = jnp.concatenate([rms_norm(o_lru, p["g_lru"]), rms_norm(o_sgu, p["g_sgu"])], axis=-1)
    return o @ p["w_out"], hf[:, -1], hb[:, 0]


def conv_ffn(h, p, grid):
    z = h @ p["ffn_up"]
    bsz, n, f2 = z.shape
    if grid:
        rows = n // GRID_W
        z = dw_conv(z.reshape(bsz, rows, GRID_W, f2), p["ffn_conv_w"], p["ffn_conv_b"],
                    FFN_CONV_LEFT).reshape(bsz, n, f2)
    else:
        z = dw_conv(z, p["ffn_conv_w"], p["ffn_conv_b"], FFN_CONV_LEFT)
    g, val = jnp.split(z, 2, axis=-1)
    return (jax.nn.silu(g) * val) @ p["ffn_down"]


def layer(x, mod, h0_fwd, h0_bwd, p, grid):
    sh1, sc1, g1, sh2, sc2, g2 = jnp.split(mod, N_MOD, axis=-1)
    h = rms_norm(x, p["norm1"]) * (1 + sc1) + sh1
    o, hf, hb = mixer(h, h0_fwd, h0_bwd, p)
    x = x + g1 * o
    h = rms_norm(x, p["norm2"]) * (1 + sc2) + sh2
    x = x + g2 * conv_ffn(h, p, grid)
    return x, hf, hb


def setup_inputs(seed: int = 0) -> dict:
    key = jax.random.key(seed)
    ks = jax.random.split(key, 32)
    nrm = jax.random.normal
    u = jax.random.uniform(ks[0], (DEPTH, 2, LRU_WIDTH), minval=0.9, maxval=0.999)
    s = u ** (1.0 / LRU_C)
    lam = jnp.log(s) - jnp.log1p(-s)
    return {
        "x_prompt": nrm(ks[1], (BATCH, SEQ, D_MODEL), jnp.float32),
        "x_sample": nrm(ks[2], (DEC_BATCH, DEC_SEQ, D_MODEL), jnp.float32),
        "state_lru": 0.5 * nrm(ks[3], (DEC_BATCH, DEPTH, 2, LRU_WIDTH), jnp.float32),
        "c": nrm(ks[4], (DEC_BATCH, D_MODEL), jnp.float32),
        "c_ctx": nrm(ks[5], (D_MODEL,), jnp.float32),
        "norm1": 1.0 + 0.05 * nrm(ks[6], (DEPTH, D_MODEL), jnp.float32),
        "norm2": 1.0 + 0.05 * nrm(ks[7], (DEPTH, D_MODEL), jnp.float32),
        "w_ada": 0.3 * D_MODEL ** -0.5 * nrm(ks[8], (DEPTH, D_MODEL, N_MOD * D_MODEL), jnp.float32),
        "b_ada": 0.01 * nrm(ks[9], (DEPTH, N_MOD * D_MODEL), jnp.float32),
        "w_in": D_MODEL ** -0.5 * nrm(ks[10], (DEPTH, D_MODEL, D_IN), jnp.float32),
        "lru_conv_w": LRU_CONV_W ** -0.5 * nrm(ks[11], (DEPTH, LRU_CONV_W, LRU_WIDTH), jnp.float32),
        "lru_conv_b": 0.01 * nrm(ks[12], (DEPTH, LRU_WIDTH), jnp.float32),
        "lru_wa": LRU_HEAD_DIM ** -0.5 * nrm(ks[13], (DEPTH, 2, LRU_HEADS, LRU_HEAD_DIM, LRU_HEAD_DIM), jnp.float32),
        "lru_ba": 0.01 * nrm(ks[14], (DEPTH, 2, LRU_WIDTH), jnp.float32),
        "lru_wx": LRU_HEAD_DIM ** -0.5 * nrm(ks[15], (DEPTH, 2, LRU_HEADS, LRU_HEAD_DIM, LRU_HEAD_DIM), jnp.float32),
        "lru_bx": 0.01 * nrm(ks[16], (DEPTH, 2, LRU_WIDTH), jnp.float32),
        "lru_lam": lam.astype(jnp.float32),
        "sgu_ws": CHUNK ** -0.5 * nrm(ks[17], (DEPTH, SGU_GROUPS, CHUNK, CHUNK), jnp.float32),
        "sgu_bs": 1.0 + 0.1 * nrm(ks[18], (DEPTH, SGU_GROUPS, CHUNK), jnp.float32),
        "g_lru": 1.0 + 0.05 * nrm(ks[19], (DEPTH, LRU_WIDTH), jnp.float32),
        "g_sgu": 1.0 + 0.05 * nrm(ks[20], (DEPTH, SGU_WIDTH), jnp.float32),
        "w_out": D_MIX ** -0.5 * nrm(ks[21], (DEPTH, D_MIX, D_MODEL), jnp.float32),
        "ffn_up": D_MODEL ** -0.5 * nrm(ks[22], (DEPTH, D_MODEL, 2 * D_FF), jnp.float32),
        "ffn_conv_w": FFN_CONV_W ** -0.5 * nrm(ks[23], (DEPTH, FFN_CONV_W, 2 * D_FF), jnp.float32),
        "ffn_conv_b": 0.01 * nrm(ks[24], (DEPTH, 2 * D_FF), jnp.float32),
        "ffn_down": D_FF ** -0.5 * nrm(ks[25], (DEPTH, D_FF, D_MODEL), jnp.float32),
        "final_norm": 1.0 + 0.05 * nrm(ks[26], (D_MODEL,), jnp.float32),
    }


def reference(x_prompt, x_sample, state_lru, c, c_ctx, norm1, norm2, w_ada, b_ada, w_in,
              lru_conv_w, lru_conv_b, lru_wa, lru_ba, lru_wx, lru_bx, lru_lam, sgu_ws, sgu_bs,
              g_lru, g_sgu, w_out, ffn_up, ffn_conv_w, ffn_conv_b, ffn_down, final_norm):
    xp = x_prompt
    xs = x_sample
    zeros = jnp.zeros((x_prompt.shape[0], LRU_WIDTH), x_prompt.dtype)
    new_states = []
    for l in range(DEPTH):
        p = {
            "norm1": norm1[l], "norm2": norm2[l], "w_in": w_in[l],
            "lru_conv_w": lru_conv_w[l], "lru_conv_b": lru_conv_b[l],
            "lru_wa": lru_wa[l], "lru_ba": lru_ba[l], "lru_wx": lru_wx[l], "lru_bx": lru_bx[l],
            "lru_lam": lru_lam[l], "sgu_ws": sgu_ws[l], "sgu_bs": sgu_bs[l],
            "g_lru": g_lru[l], "g_sgu": g_sgu[l], "w_out": w_out[l],
            "ffn_up": ffn_up[l], "ffn_conv_w": ffn_conv_w[l], "ffn_conv_b": ffn_conv_b[l],
            "ffn_down": ffn_down[l],
        }
        mod_ctx = (jax.nn.silu(c_ctx) @ w_ada[l] + b_ada[l])[None, None, :]
        mod_lat = (jax.nn.silu(c) @ w_ada[l] + b_ada[l])[:, None, :]
        xp, hf, hb = layer(xp, mod_ctx, zeros, zeros, p, False)
        new_states.append(jnp.stack([hf, hb], axis=1))
        xs, _, _ = layer(xs, mod_lat, state_lru[:, l, 0], state_lru[:, l, 1], p, True)
    y_prompt = rms_norm(xp, final_norm)
    y_sample = rms_norm(xs, final_norm)
    new_state_lru = jnp.stack(new_states, axis=1)
    return (y_prompt, y_sample, new_state_lru)
```

```python
import numpy as np
from contextlib import ExitStack
import concourse.bass as bass
import concourse.mybir as mybir
from concourse.bass_utils import run_bass_kernel_spmd

F32 = mybir.dt.float32
BF16 = mybir.dt.bfloat16
AF = mybir.ActivationFunctionType
ALU = mybir.AluOpType

D = 1024
T = 512
NSLOT = 6
EPS = 1e-6

_off = {}
_n = 0
for _name, _w in [("norm1", 8), ("norm2", 8), ("fnorm", 8), ("bada", 48), ("cw", 20), ("cb", 4),
                  ("ba", 8), ("bx", 8), ("lam", 8), ("h0", 8), ("glru", 4), ("gsgu", 4),
                  ("fw", 144), ("fb", 48), ("eps", 1), ("one", 1), ("oneg", 1)]:
    _off[_name] = _n
    _n += _w
NPP = _n


class Op:
    __slots__ = ("eng", "emit", "deps", "dma", "signal", "count", "waits", "idx")


class Prog:
    def __init__(self):
        self.ops = []
        self.lastw = {}
        self.readers = {}

    def op(self, eng, emit, reads=(), writes=(), dma=None):
        o = Op()
        o.eng, o.emit, o.dma, o.signal, o.count, o.waits = eng, emit, dma, False, 0, []
        o.idx = len(self.ops)
        deps = {}
        for r in reads:
            w = self.lastw.get(r)
            if w is not None:
                deps[w] = "raw"
        for w_ in writes:
            lw = self.lastw.get(w_)
            if lw is not None and lw not in deps:
                deps[lw] = "waw"
            for rd in self.readers.get(w_, ()):
                if rd not in deps:
                    deps[rd] = "war"
        for r in reads:
            self.readers.setdefault(r, []).append(o.idx)
        for w_ in writes:
            self.lastw[w_] = o.idx
            self.readers[w_] = []
        o.deps = deps
        self.ops.append(o)
        return o

    def finalize(self, nc, final_wait_keys):
        ops = self.ops
        for o in ops:
            for d, kind in o.deps.items():
                do = ops[d]
                if do.dma is None and o.dma is None and do.eng == o.eng:
                    if o.eng == "pe":
                        continue
                do.signal = True
                o.waits.append(d)
        for o in ops:
            if o.dma is not None and o.dma in final_wait_keys:
                o.signal = True
        cnt = {}
        for o in ops:
            if not o.signal:
                continue
            key = ("dma", o.dma) if o.dma is not None else ("eng", o.eng)
            cnt[key] = cnt.get(key, 0) + (16 if o.dma is not None else 1)
            o.count = cnt[key]
        self.totals = cnt
        self.semkeys = list(cnt.keys())

    def emit_all(self, nc, block, sems, final_wait_keys):
        engs = {"pe": "tensor", "act": "scalar", "dve": "vector", "pool": "gpsimd", "sp": "sync"}
        ops = self.ops
        totals = self.totals

        def target(d):
            do = ops[d]
            if do.dma is not None:
                key = ("dma", do.dma)
                val = totals[key] if isinstance(do.dma, str) and do.dma.startswith("const") else do.count
                return key, val
            return ("eng", do.eng), do.count

        for ename, bname in engs.items():
            mine = [o for o in ops if o.eng == ename]

            def body(e, mine=mine, ename=ename):
                waited = {}
                for o in mine:
                    tg = {}
                    for d in o.waits:
                        k, v = target(d)
                        if v > tg.get(k, 0):
                            tg[k] = v
                    for k, v in tg.items():
                        if waited.get(k, 0) < v:
                            e.wait_ge(sems[k], v)
                            waited[k] = v
                    ins = o.emit(e)
                    if o.signal:
                        key = ("dma", o.dma) if o.dma is not None else ("eng", o.eng)
                        ins.then_inc(sems[key], 16 if o.dma is not None else 1)
                if ename == "sp":
                    for k in final_wait_keys:
                        kk = ("dma", k)
                        if kk in totals:
                            e.wait_ge(sems[kk], totals[kk])

            getattr(block, bname)(body)


def build_program():
    ws = _build(None)
    return _build(ws)


def _build(wsched):
    nc = bass.Bass("TRN2", target_bir_lowering=False)
    P = Prog()
    dry = wsched is None
    wreq = []

    def dram(name, shape, kind="ExternalInput", dt=F32):
        return nc.dram_tensor(name, list(shape), dt, kind=kind).ap()

    xs_d = dram("xs", [D, 2048])
    xp_d = dram("xp", [D, 512])
    pp_d = dram("pp", [128, NPP])
    cvec_d = dram("cvec", [128, 16])
    bsrow_d = dram("bsrow", [1, 512])
    gw_d = dram("gw", [128, 16 * 128])
    ws_d = dram("wsT", [128, 4 * 128])
    wada_d = dram("wada", [24, 128, 2048])
    win_d = dram("win", [8, 128, 2048])
    wout_d = dram("wout", [4, 128, 2048])
    wup_d = dram("wup", [24, 128, 2048])
    wdn_d = dram("wdn", [16, 128, 1536])
    ys_d = dram("ys", [D, 1024], kind="ExternalOutput")
    yp_d = dram("yp", [D, 512], kind="ExternalOutput")
    nst_d = dram("nst", [128, 16], kind="ExternalOutput")

    es = ExitStack()

    def sb(name, shape, dt=F32):
        return es.enter_context(nc.sbuf_tensor("sb_" + name, list(shape), dt))

    pp = sb("pp", [128, NPP])
    cvec = sb("cvec", [128, 16])
    csil = sb("csil", [128, 16], BF16)
    ones_bf = sb("ones_bf", [128, 128], BF16)
    ones_row = sb("ones_row", [1, 128])
    bsrow = sb("bsrow", [1, 512])
    gw = sb("gw", [128, 16, 128], BF16)
    wsT = sb("wsT", [128, 4, 128], BF16)
    bsbc = sb("bsbc", [128, 4, 128])
    modT = sb("modT", [128, 48, 2])
    w1 = sb("w1", [128, 8, 2])
    w2 = sb("w2", [128, 8, 2])
    lamc = sb("lamc", [128, 8])
    lamh = sb("lamh", [128, 8])
    hba = sb("hba", [128, 8])
    hbx = sb("hbx", [128, 8])
    tmpl = sb("tmpl", [128, 8])
    carry = sb("carry", [128, 8])
    nst = sb("nst", [128, 16])
    wring = sb("wring", [128, NSLOT, 2048], BF16)
    NS = 41
    S = sb("S", [128, NS, T])
    NH = 56
    H = sb("H", [128, NH, T], BF16)
    A = sb("A", [128, 4, 2052])
    ps = es.enter_context(nc.psum_tensor("psum_all", [128, 8, T], F32))

    XT = 0
    GU, OS, NT, SR = 8, 12, 16, 18
    FT = 8
    LR = 19
    AS, A2, TI, TR = LR, LR + 4, LR + 8, LR + 12
    GY, OL = LR, LR + 2
    XC = 33
    HID, SQ, OO, GV = 0, 0, 8, 16
    H2 = 24
    HK = 32
    XCB = 48

    def pcol(name, i=0):
        c = _off[name] + i
        return pp[:, c:c + 1]

    state = {"bank": 0, "sbank": 0, "wslot": 0, "wissued": 0}

    def bank():
        b = state["bank"]
        state["bank"] = (b + 1) % 6
        return b

    def sbank():
        b = 6 + state["sbank"]
        state["sbank"] = 1 - state["sbank"]
        return b

    def mm(out, lhsT, rhs, start, stop, reads, writes):
        P.op("pe", lambda e: e.matmul(out, lhsT=lhsT, rhs=rhs, start=start, stop=stop), reads, writes)

    def act(out, in_, func, reads, writes, bias=None, scale=None):
        kw = {}
        if bias is not None:
            kw["bias"] = bias
        if scale is not None:
            kw["scale"] = scale
        P.op("act", lambda e: e.activation(out=out, in_=in_, func=func, **kw), reads, writes)

    def stt(out, in0, scalar, in1, op0, op1, reads, writes):
        P.op("dve", lambda e: e.scalar_tensor_tensor(out=out, in0=in0, scalar=scalar, in1=in1, op0=op0, op1=op1),
             reads, writes)

    def tt(out, in0, in1, op, reads, writes):
        P.op("dve", lambda e: e.tensor_tensor(out=out, in0=in0, in1=in1, op=op), reads, writes)

    def ts(out, in0, s1, s2, op0, op1, reads, writes):
        if s2 is None:
            P.op("dve", lambda e: e.tensor_scalar(out=out, in0=in0, scalar1=s1, scalar2=None, op0=op0), reads, writes)
        else:
            P.op("dve", lambda e: e.tensor_scalar(out=out, in0=in0, scalar1=s1, scalar2=s2, op0=op0, op1=op1),
                 reads, writes)

    def ptt(out, in0, in1, op, reads, writes):
        tt(out, in0, in1, op, reads, writes)

    def psq(out, in_, reads, writes):
        act(out, in_, AF.Square, reads, writes)

    def copy_col(dst, src, reads, writes):
        P.op("dve", lambda e: e.tensor_copy(out=dst, in_=src), reads, writes)

    def _wissue(u, src_ap, nel):
        s = u % NSLOT
        P.op("pool", lambda e: e.dma_start(out=wring[:, s, 0:nel], in_=src_ap), (), [("w", s)], dma=("w", s))

    def wload(src_ap, nel):
        i = state["wslot"]
        state["wslot"] = i + 1
        if dry:
            wreq.append((src_ap, nel))
            _wissue(i, src_ap, nel)
            return i % NSLOT
        while state["wissued"] < min(len(wsched), i + NSLOT - 1):
            u = state["wissued"]
            _wissue(u, *wsched[u])
            state["wissued"] = u + 1
        return i % NSLOT

    def drain(g):
        for _ in g:
            pass

    def interleave(main, side, rate):
        acc = 0.0
        ds = False
        for tag in main:
            acc += rate.get(tag, rate[None])
            while acc >= 1.0 and not ds:
                acc -= 1.0
                try:
                    next(side)
                except StopIteration:
                    ds = True
        for _ in side:
            pass

    P.op("sp", lambda e: e.dma_start(out=pp[:], in_=pp_d), (), ["pp"], dma="const")
    P.op("sp", lambda e: e.dma_start(out=cvec[:], in_=cvec_d), (), ["cvec"], dma="const")
    P.op("sp", lambda e: e.dma_start(out=bsrow[:], in_=bsrow_d), (), ["bsrow"], dma="const")
    P.op("pool", lambda e: e.dma_start(out=gw[:].rearrange("p a b -> p (a b)"), in_=gw_d), (), ["gw"], dma="constb")
    P.op("pool", lambda e: e.dma_start(out=wsT[:].rearrange("p a b -> p (a b)"), in_=ws_d), (), ["wsT"], dma="constb")
    P.op("dve", lambda e: e.memset(ones_bf[:], 1.0), (), ["ones_bf"])
    P.op("dve", lambda e: e.memset(ones_row[:], 1.0), (), ["ones_row"])
    allA = [("A", c, t) for c in range(4) for t in range(4)]
    P.op("dve", lambda e: e.memset(A[:], 0.0), (), allA + ["Apad"])
    P.op("dve", lambda e: e.memset(carry[:], 0.0), (), ["carry"])

    act(csil[:], cvec[:], AF.Silu, ["cvec"], ["csil"])
    lam_ap = pp[:, _off["lam"]:_off["lam"] + 8]
    act(tmpl[:], lam_ap, AF.Exp, ["pp"], ["tmpl"], scale=-1.0)
    act(tmpl[:], tmpl[:], AF.Ln, ["tmpl"], ["tmpl"], bias=pcol("one"))
    ts(lamc[:], tmpl[:], -8.0, None, ALU.mult, None, ["tmpl"], ["lamc"])
    ts(lamh[:], tmpl[:], -4.0, None, ALU.mult, None, ["tmpl"], ["lamh"])
    ts(hba[:], pp[:, _off["ba"]:_off["ba"] + 8], 0.5, None, ALU.mult, None, ["pp"], ["hba"])
    ts(hbx[:], pp[:, _off["bx"]:_off["bx"] + 8], 0.5, None, ALU.mult, None, ["pp"], ["hbx"])

    _bb = bank()
    for g in range(4):
        mm(ps[:, _bb, g * 128:g * 128 + 128], ones_row[0:1, :], bsrow[0:1, g * 128:g * 128 + 128], True, True,
           ["ones_row", "bsrow"], [("ps", _bb)])
    act(bsbc[:].rearrange("p g q -> p (g q)"), ps[:, _bb, :], AF.Copy, [("ps", _bb)], ["bsbc"])


    def mod_part(u0, u1):
        MODB = bank()
        for u in range(u0, u1):
            s = wload(wada_d[u], 2048)
            for mmi in range(2):
                j = u * 2 + mmi
                for k in range(8):
                    mm(ps[:, MODB, 2 * j:2 * j + 2], wring[:, s, k * 256 + mmi * 128:k * 256 + mmi * 128 + 128],
                       csil[:, 2 * k:2 * k + 2], k == 0, k == 7, [("w", s), "csil"], [("ps", MODB)])
        j0, j1 = u0 * 2, u1 * 2
        for v in range(2):
            tt(modT[:, j0:j1, v], ps[:, MODB, 0:96].rearrange("p (j v) -> p j v", v=2)[:, j0:j1, v],
               pp[:, _off["bada"] + j0:_off["bada"] + j1], ALU.add, [("ps", MODB), "pp"], [("mod", v, j0)])

    MOD2 = [(8, 12), (12, 16), (16, 20), (20, 24)]

    def mod2keys(v):
        return [("mod", v, 2 * u0) for (u0, u1) in MOD2]

    def mod1():
        mod_part(0, 8)
        for v in range(2):
            stt(w1[:, :, v], modT[:, 8:16, v], 1.0, pp[:, _off["norm1"]:_off["norm1"] + 8], ALU.add, ALU.mult,
                [("mod", v, 0), "pp"], [("w1", v)])

    def load_x(src_d, col0, xb=None):
        xb = XT if xb is None else xb
        P.op("sp", lambda e: e.dma_start(out=S[:, xb:xb + 8, :],
                                         in_=src_d.rearrange("(k p) t -> p k t", p=128)[:, :, col0:col0 + T]),
             (), [("S", xb + k) for k in range(8)], dma=("xload", xb))

    def rstd_bank(sq0, nk, dim):
        b0 = sbank()
        for k in range(nk):
            mm(ps[:, b0, :], ones_bf[:], H[:, sq0 + k, :], k == 0, k == nk - 1, ["ones_bf", ("H", sq0 + k)], [("ps", b0)])
        act(ps[:, b0, :], ps[:, b0, :], AF.Ln, [("ps", b0), "pp"], [("ps", b0)], bias=pcol("eps"), scale=1.0 / dim)
        act(ps[:, b0, :], ps[:, b0, :], AF.Exp, [("ps", b0)], [("ps", b0)], scale=-0.5)
        return b0

    def norm_front(xb, sq0=None, nrot=8):
        sq0 = SQ if sq0 is None else sq0
        b0 = sbank()
        for k in range(8):
            sl = sq0 + (k % nrot)
            psq(H[:, sl, :], S[:, xb + k, :], [("S", xb + k)], [("H", sl)])
            mm(ps[:, b0, :], ones_bf[:], H[:, sl, :], k == 0, k == 7, ["ones_bf", ("H", sl)], [("ps", b0)])
        act(ps[:, b0, :], ps[:, b0, :], AF.Ln, [("ps", b0), "pp"], [("ps", b0)], bias=pcol("eps"), scale=1.0 / float(D))
        act(ps[:, b0, :], ps[:, b0, :], AF.Exp, [("ps", b0)], [("ps", b0)], scale=-0.5)
        return b0

    def norm_back(xb, rb, wv, shv, v, hslot, modkeys, dve_bias=False):
        for k in range(8):
            tmp = NT + (k % 2)
            stt(S[:, tmp, :], S[:, xb + k, :], wv[:, k, v:v + 1], ps[:, rb, :], ALU.mult, ALU.mult,
                [("S", xb + k), ("ps", rb)] + modkeys, [("S", tmp)])
            if dve_bias:
                ts(H[:, hslot + k, :], S[:, tmp, :], shv[:, k, v:v + 1], None, ALU.add, None,
                   [("S", tmp)] + modkeys, [("H", hslot + k)])
            else:
                act(H[:, hslot + k, :], S[:, tmp, :], AF.Identity, [("S", tmp)] + modkeys, [("H", hslot + k)],
                    bias=shv[:, k, v:v + 1])

    def norm_mod(wv, shv, v, hslot, modkeys):
        rb = norm_front(XT)
        norm_back(XT, rb, wv, shv, v, hslot, modkeys)

    def proj_in(unit0, hslot, m, evac):
        s = unit0[m // 2]
        b = bank()
        for k in range(8):
            mm(ps[:, b, :], wring[:, s, k * 256 + (m % 2) * 128:k * 256 + (m % 2) * 128 + 128], H[:, hslot + k, :],
               k == 0, k == 7, [("w", s), ("H", hslot + k)], [("ps", b)])
        evac(b)

    def xr_front(src_d, col0, xb, sq0=None, nrot=8):
        load_x(src_d, col0, xb)
        return norm_front(xb, sq0, nrot)

    def xr_back_a(xb, rb, v, hslot, slots=None):
        norm_back(xb, rb, w1, modT[:, 0:8, :], v, hslot, [("w1", v), ("mod", v, 0)], dve_bias=True)
        lazy = slots is None
        if lazy:
            slots = [None, None]
        if lazy:
            slots = [wload(win_d[0], 2048), wload(win_d[1], 2048)]
        banks = [bank() for _ in range(4)]
        for k in range(8):
            for m in range(4):
                s_ = slots[m // 2]
                mm(ps[:, banks[m], :], wring[:, s_, k * 256 + (m % 2) * 128:k * 256 + (m % 2) * 128 + 128],
                   H[:, hslot + k, :], k == 0, k == 7, [("w", s_), ("H", hslot + k)], [("ps", banks[m])])
        return banks

    def xr_back_b(banks, akeys, seqs):
        for m, b in enumerate(banks):
            for (tc, ln, ac) in seqs:
                act(A[:, m, ac:ac + ln], ps[:, b, tc:tc + ln], AF.Copy, [("ps", b)], [("A", m, t_) for t_ in akeys])

    def xr_back(xb, rb, v, hslot, akeys, seqs, slots=None):
        xr_back_b(xr_back_a(xb, rb, v, hslot, slots), akeys, seqs)

    def xr_pass(src_d, col0, v, hslot, akeys, seqs):
        rb = xr_front(src_d, col0, XT)
        xr_back(XT, rb, v, hslot, akeys, seqs)

    def conv_tile(akeys, segs, xc_slot, xset=0):
        two = len(segs) == 2
        if two:
            (_, ln0, ac0), (_, _, ac1) = segs
            pitch = ac1 - ac0
        def src(c, tap):
            if two:
                return A[:, c, ac0 - 2 + tap:ac0 - 2 + tap + 2 * pitch].rearrange("p (s l) -> p s l", l=pitch)[:, :, 0:ln0]
            (tc, ln, ac) = segs[0]
            return A[:, c, ac - 2 + tap:ac - 2 + tap + ln]

        def dstv(c):
            return S[:, xc_slot + c, :].rearrange("p (s l) -> p s l", l=ln0) if two else S[:, xc_slot + c, :]

        for c0 in (0, 2):
            cs = (c0, c0 + 1)
            for c in cs:
                rk = ["pp", "Apad"] + [("A", c, t) for t in akeys]
                ts(dstv(c), src(c, 2), pcol("cw", 2 * 4 + c), pcol("cb", c), ALU.mult, ALU.add, rk, [("S", xc_slot + c)])
            for tap in (0, 1, 3, 4):
                for c in cs:
                    rk = ["pp", "Apad"] + [("A", c, t) for t in akeys]
                    stt(dstv(c), src(c, tap), pcol("cw", tap * 4 + c), dstv(c), ALU.mult, ALU.add,
                        rk + [("S", xc_slot + c)], [("S", xc_slot + c)])
            for c in cs:
                xb_ = XCB + 4 * xset + c
                P.op("dve", lambda e, c=c, xb_=xb_: e.tensor_copy(out=H[:, xb_, :], in_=S[:, xc_slot + c, :]),
                     [("S", xc_slot + c)], [("H", xb_)])
                yield

    lru_state = {"set": 0}

    def lru_dir(d, xc_slot, segs_scan, out_fn, out_keys_fn, post_fn=None, tmp_out=False, xset=0):
        st0 = lru_state["set"]
        bases = [LR + 7 * st0, LR + 7 * (1 - st0)]
        for half in range(2):
            base = bases[half]
            AS_, A2_, TI_, TR_ = base, base + 2, base + 4, base + 6
            cs = (2 * half, 2 * half + 1)
            for i, c in enumerate(cs):
                xb_ = XCB + 4 * xset + c
                br, bi = bank(), bank()
                mm(ps[:, br, :], gw[:, d * 8 + c, :], H[:, xb_, :], True, True, ["gw", ("H", xb_)], [("ps", br)])
                mm(ps[:, bi, :], gw[:, d * 8 + 4 + c, :], H[:, xb_, :], True, True, ["gw", ("H", xb_)], [("ps", bi)])
                dc = d * 4 + c
                act(S[:, TR_, :], ps[:, br, :], AF.Tanh, [("ps", br), "hba"], [("S", TR_)], bias=hba[:, dc:dc + 1], scale=0.5)
                act(S[:, TI_ + i, :], ps[:, bi, :], AF.Tanh, [("ps", bi), "hbx"], [("S", TI_ + i)], bias=hbx[:, dc:dc + 1], scale=0.5)
                act(S[:, AS_ + i, :], S[:, TR_, :], AF.Exp, [("S", TR_), "lamh"], [("S", AS_ + i)],
                    bias=lamh[:, dc:dc + 1], scale=lamh[:, dc:dc + 1])
                act(S[:, A2_ + i, :], S[:, TR_, :], AF.Exp, [("S", TR_), "lamc"], [("S", A2_ + i)],
                    bias=lamc[:, dc:dc + 1], scale=lamc[:, dc:dc + 1])
            for i, c in enumerate(cs):
                stt(S[:, TI_ + i, :], S[:, TI_ + i, :], 1.0, S[:, xc_slot + c, :], ALU.add, ALU.mult,
                    [("S", TI_ + i), ("S", xc_slot + c)], [("S", TI_ + i)])
            yield
        for half in range(2):
            A2_ = bases[half] + 2
            for i in range(2):
                act(S[:, A2_ + i, :], S[:, A2_ + i, :], AF.Sqrt, [("S", A2_ + i), "pp"], [("S", A2_ + i)],
                    bias=pcol("oneg"), scale=-1.0)
        yield
        for half in range(2):
            base = bases[half]
            AS_, A2_, TI_, TR_ = base, base + 2, base + 4, base + 6
            cs = (2 * half, 2 * half + 1)
            for i, c in enumerate(cs):
                stt(S[:, TI_ + i, :], S[:, A2_ + i, :], 0.5, S[:, TI_ + i, :], ALU.mult, ALU.mult,
                    [("S", TI_ + i), ("S", A2_ + i)], [("S", TI_ + i)])
            for i, c in enumerate(cs):
                if tmp_out:
                    dst = S[:, A2_ + i, :]
                    okeys = [("S", A2_ + i)]
                else:
                    dst = out_fn(c)
                    okeys = out_keys_fn(c)
                for (tc, ln, init) in segs_scan:
                    ini = init(c) if callable(init) else init
                    rd = [("S", AS_ + i), ("S", TI_ + i), "carry", "pp"] + [("A", c, t) for t in range(4)]
                    if d == 0:
                        P.op("dve", lambda e, dst=dst, a_=AS_ + i, b_=TI_ + i, tc=tc, ln=ln, ini=ini: e.tensor_tensor_scan(
                            out=dst[:, tc:tc + ln], data0=S[:, a_, tc:tc + ln], data1=S[:, b_, tc:tc + ln],
                            initial=ini, op0=ALU.mult, op1=ALU.add), rd, okeys)
                    else:
                        P.op("dve", lambda e, dst=dst, a_=AS_ + i, b_=TI_ + i, tc=tc, ln=ln, ini=ini: e.tensor_tensor_scan(
                            out=dst[:, tc:tc + ln][:, ::-1], data0=S[:, a_, tc:tc + ln][:, ::-1],
                            data1=S[:, b_, tc:tc + ln][:, ::-1],
                            initial=ini, op0=ALU.mult, op1=ALU.add), rd, okeys)
                if post_fn is not None:
                    post_fn(c, dst, okeys)
            yield

    def phase_b1(hslot, hs_fn, hs_keys_fn, finish=True):
        bks = []
        for m in range(4):
            if m % 2 == 0:
                s_ = wload(win_d[2 + m // 2], 2048)
            b = bank()
            for k in range(8):
                mm(ps[:, b, :], wring[:, s_, k * 256 + (m % 2) * 128:k * 256 + (m % 2) * 128 + 128], H[:, hslot + k, :],
                   k == 0, k == 7, [("w", s_), ("H", hslot + k)], [("ps", b)])
            bks.append(b)
        GY4 = GY
        for m, b in enumerate(bks):
            act(S[:, OL + m, :], ps[:, b, :], AF.Gelu_apprx_tanh, [("ps", b)], [("S", OL + m)])
        for m in range(4):
            tt(S[:, OL + m, :], S[:, OL + m, :], hs_fn(m), ALU.mult, [("S", OL + m)] + hs_keys_fn(m), [("S", OL + m)])
        for m in range(4):
            psq(H[:, SQ + m, :], S[:, OL + m, :], [("S", OL + m)], [("H", SQ + m)])
        yield
        if finish:
            b1_finish()
        yield

    def b1_finish():
        rb_l = rstd_bank(SQ, 4, 512.0)
        for m in range(4):
            stt(H[:, OO + m, :], S[:, OL + m, :], pcol("glru", m), ps[:, rb_l, :], ALU.mult, ALU.mult,
                [("S", OL + m), ("ps", rb_l), "pp"], [("H", OO + m)])

    def phase_b2(x_src, x_col0, reload_x, hslot, v, rowlen, y_d, y_col0, hooks=(), after_v=None):
        hooks = list(hooks)
        g1 = modT[:, 16:24, :]
        sh2 = modT[:, 24:32, :]
        g2 = modT[:, 40:48, :]
        mk = mod2keys(v)
        if reload_x:
            load_x(x_src, x_col0)
        bks = []
        for m in range(4):
            if m % 2 == 0:
                s_ = wload(win_d[4 + m // 2], 2048)
            b = bank()
            for k in range(8):
                mm(ps[:, b, :], wring[:, s_, k * 256 + (m % 2) * 128:k * 256 + (m % 2) * 128 + 128], H[:, hslot + k, :],
                   k == 0, k == 7, [("w", s_), ("H", hslot + k)], [("ps", b)])
            bks.append(b)
        for m, b in enumerate(bks):
            act(S[:, GU + m, :], ps[:, b, :], AF.Gelu_apprx_tanh, [("ps", b)], [("S", GU + m)])
        yield
        slots = [wload(win_d[6], 2048), wload(win_d[7], 2048)]
        bks = []
        for n in range(4):
            b = bank()
            for hf_ in range(2):
                s = slots[hf_]
                for k in range(8):
                    mm(ps[:, b, hf_ * 256:hf_ * 256 + 256], H[:, hslot + k, n * 128:n * 128 + 128],
                       wring[:, s, k * 256:k * 256 + 256], k == 0, k == 7, [("w", s), ("H", hslot + k)], [("ps", b)])
            bks.append(b)
        for n, b in enumerate(bks):
            act(H[:, GV + n, :], ps[:, b, :], AF.Gelu_apprx_tanh, [("ps", b)], [("H", GV + n)])
        yield
        sb_ = [bank() for _ in range(4)]
        for g in range(4):
            for n in range(4):
                mm(ps[:, sb_[g], n * 128:n * 128 + 128], H[:, GV + n, g * 128:g * 128 + 128], wsT[:, g, :], True, True,
                   [("H", GV + n), "wsT"], [("ps", sb_[g])])
        for g in range(4):
            tt(S[:, OS + g, :].rearrange("p (n q) -> p n q", q=128), ps[:, sb_[g], :].rearrange("p (n q) -> p n q", q=128),
               bsbc[:, g, :].unsqueeze(1).to_broadcast([128, 4, 128]), ALU.add, [("ps", sb_[g]), "bsbc"], [("S", OS + g)])
        for g in range(4):
            tt(S[:, OS + g, :], S[:, OS + g, :], S[:, GU + g, :], ALU.mult, [("S", OS + g), ("S", GU + g)], [("S", OS + g)])
        for g in range(4):
            psq(H[:, SQ + 4 + g, :], S[:, OS + g, :], [("S", OS + g)], [("H", SQ + 4 + g)])
        if after_v is not None:
            after_v()
        rb_s = rstd_bank(SQ + 4, 4, 512.0)
        for m in range(4):
            stt(H[:, OO + 4 + m, :], S[:, OS + m, :], pcol("gsgu", m), ps[:, rb_s, :], ALU.mult, ALU.mult,
                [("S", OS + m), ("ps", rb_s), "pp"], [("H", OO + 4 + m)])
        yield
        for m in range(8):
            if m % 2 == 0:
                s = wload(wout_d[m // 2], 2048)
            b = bank()
            for k in range(8):
                mm(ps[:, b, :], wring[:, s, k * 256 + (m % 2) * 128:k * 256 + (m % 2) * 128 + 128], H[:, OO + k, :],
                   k == 0, k == 7, [("w", s), ("H", OO + k)], [("ps", b)])
            stt(S[:, XT + m, :], ps[:, b, :], g1[:, m, v:v + 1], S[:, XT + m, :], ALU.mult, ALU.add,
                [("ps", b), ("S", XT + m)] + mk, [("S", XT + m)])
            yield
        norm_mod(w2, sh2, v, H2, [("w2", v)] + mk)
        yield
        pend = None
        for j in range(24):
            s = wload(wup_d[j], 2048)
            bg, bv = bank(), bank()
            for k in range(8):
                mm(ps[:, bg, :], wring[:, s, k * 256:k * 256 + 128], H[:, H2 + k, :], k == 0, k == 7,
                   [("w", s), ("H", H2 + k)], [("ps", bg)])
            for k in range(8):
                mm(ps[:, bv, :], wring[:, s, k * 256 + 128:k * 256 + 256], H[:, H2 + k, :], k == 0, k == 7,
                   [("w", s), ("H", H2 + k)], [("ps", bv)])
            o3 = (j % 2) * 3
            for (bk, fc, tslot) in ((bg, j, FT + o3), (bv, 24 + j, FT + o3 + 1)):
                act(S[:, tslot, :], ps[:, bk, :], AF.Identity, [("ps", bk), "pp"], [("S", tslot)],
                    bias=pcol("fb", fc), scale=pcol("fw", 48 + fc))
                pv = ps[:, bk, :].rearrange("p (r l) -> p r l", l=rowlen)
                sv = S[:, tslot, :].rearrange("p (r l) -> p r l", l=rowlen)
                stt(sv[:, :, 1:rowlen], pv[:, :, 0:rowlen - 1], pcol("fw", fc), sv[:, :, 1:rowlen], ALU.mult, ALU.add,
                    [("ps", bk), ("S", tslot), "pp"], [("S", tslot)])
                stt(sv[:, :, 0:rowlen - 1], pv[:, :, 1:rowlen], pcol("fw", 96 + fc), sv[:, :, 0:rowlen - 1], ALU.mult, ALU.add,
                    [("ps", bk), ("S", tslot), "pp"], [("S", tslot)])
            if pend is not None:
                pend()
            sg = FT + o3 + 2
            act(S[:, sg, :], S[:, FT + o3, :], AF.Silu, [("S", FT + o3)], [("S", sg)])

            def pend(j=j, sg=sg, o3=o3):
                ptt(H[:, HID + j, :], S[:, sg, :], S[:, FT + o3 + 1, :], ALU.mult, [("S", sg), ("S", FT + o3 + 1)],
                    [("H", HID + j)])
            yield "f"
        pend()
        for m in range(8):
            s0 = wload(wdn_d[2 * m], 1536)
            s1 = wload(wdn_d[2 * m + 1], 1536)
            b = bank()
            for k in range(24):
                s = s0 if k < 12 else s1
                kk = k % 12
                mm(ps[:, b, :], wring[:, s, kk * 128:kk * 128 + 128], H[:, HID + k, :], k == 0, k == 23,
                   [("w", s), ("H", HID + k)], [("ps", b)])
            stt(S[:, XT + m, :], ps[:, b, :], g2[:, m, v:v + 1], S[:, XT + m, :], ALU.mult, ALU.add,
                [("ps", b), ("S", XT + m)] + mk, [("S", XT + m)])
            yield "d"
        for k in range(8):
            psq(H[:, SQ + k, :], S[:, XT + k, :], [("S", XT + k)], [("H", SQ + k)])
        rb = rstd_bank(SQ, 8, float(D))
        for k in range(8):
            stt(S[:, XT + k, :], S[:, XT + k, :], pcol("fnorm", k), ps[:, rb, :], ALU.mult, ALU.mult,
                [("S", XT + k), ("ps", rb), "pp"], [("S", XT + k)])
        P.op("sp", lambda e: e.dma_start(out=y_d.rearrange("(k p) t -> p k t", p=128)[:, :, y_col0:y_col0 + T],
                                         in_=S[:, XT:XT + 8, :]),
             [("S", XT + k) for k in range(8)], [], dma="store")
        yield

    def h0col(d):
        return lambda c: pp[:, _off["h0"] + d * 4 + c:_off["h0"] + d * 4 + c + 1]

    def carrycol(d):
        return lambda c: carry[:, d * 4 + c:d * 4 + c + 1]

    def acols(t):
        return slice(2 + t * T, 2 + (t + 1) * T)

    P.op("dve", lambda e: e.memset(A[:, :, 258:262], 0.0), (), allA)
    P.op("dve", lambda e: e.memset(A[:, :, 518:520], 0.0), (), allA)
    pseg = [(0, 256, 2), (256, 256, 262)]
    rb_p = xr_front(xp_d, 0, XT)
    mod1()
    xr_back(XT, rb_p, 1, H2, [0, 1], pseg)
    drain(conv_tile([0, 1], pseg, XC))
    mod_part(*MOD2[0])
    drain(lru_dir(0, XC, [(0, 256, 0.0), (256, 256, 0.0)], lambda c: A[:, c, acols(2)], lambda c: [("A", c, 2)]))
    mod_part(*MOD2[1])
    drain(lru_dir(1, XC, [(0, 256, 0.0), (256, 256, 0.0)], lambda c: A[:, c, acols(3)], lambda c: [("A", c, 3)]))
    mod_part(*MOD2[2])
    for sq_ in range(2):
        for c in range(4):
            cf = 2 + 2 * T + sq_ * 256 + 255
            cb_ = 2 + 3 * T + sq_ * 256
            copy_col(nst[:, sq_ * 8 + c:sq_ * 8 + c + 1], A[:, c, cf:cf + 1], [("A", c, 2)], ["nst"])
            copy_col(nst[:, sq_ * 8 + 4 + c:sq_ * 8 + 4 + c + 1], A[:, c, cb_:cb_ + 1], [("A", c, 3)], ["nst"])
    P.op("sp", lambda e: e.dma_start(out=nst_d, in_=nst[:]), ["nst"], [], dma="store2")
    for c in range(4):
        tt(A[:, c, acols(2)], A[:, c, acols(2)], A[:, c, acols(3)], ALU.add, [("A", c, 2), ("A", c, 3)], [("A", c, 2)])
    drain(phase_b1(H2, lambda m: A[:, m, acols(2)], lambda m: [("A", m, 2)]))
    mod_part(*MOD2[3])
    for v in range(2):
        stt(w2[:, :, v], modT[:, 32:40, v], 1.0, pp[:, _off["norm2"]:_off["norm2"] + 8], ALU.add, ALU.mult,
            mod2keys(v) + ["pp"], [("w2", v)])

    def merge(*gens):
        gens = list(gens)
        while gens:
            for g in list(gens):
                try:
                    next(g)
                    yield
                except StopIteration:
                    gens.remove(g)

    def sample_front(last=False):

        def convg(t, slot, xs):
            return conv_tile([max(t - 1, 0), t, min(t + 1, 3)], [(0, T, 2 + t * T)], slot, xs)

        def bwd_other(t, slot, xs):
            init = h0col(1) if t == 3 else carrycol(1)

            def post(c, dst, okeys):
                copy_col(carry[:, 4 + c:5 + c], dst[:, 0:1], okeys, ["carry"])
            return lru_dir(1, slot, [(0, T, init)], None, None, post, tmp_out=True, xset=xs)

        def fwd_own(t, slot, xs):
            init = h0col(0) if t == 0 else (lambda c: A[:, c, 2 + T - 1:2 + T])
            return lru_dir(0, slot, [(0, T, init)], lambda c: A[:, c, acols(t)], lambda c: [("A", c, t)], xset=xs)

        def bwd_own(t, slot, xs):
            init = carrycol(1) if t == 1 else (lambda c: A[:, c, 2 + 3 * T:2 + 3 * T + 1])
            return lru_dir(1, slot, [(0, T, init)], lambda c: A[:, c, acols(2 + t)], lambda c: [("A", c, 2 + t)], xset=xs)

        yield from convg(3, XC, 0)
        yield from merge(bwd_other(3, XC, 0), convg(2, XC + 4, 1))
        yield from merge(bwd_other(2, XC + 4, 1), convg(0, XC, 0))
        yield from merge(fwd_own(0, XC, 0), convg(1, XC + 4, 1))
        yield from fwd_own(1, XC + 4, 1)
        yield from bwd_own(1, XC + 4, 1)
        if not last:
            yield from hs_add(1)
            return
        yield from bwd_own(0, XC, 0)
        yield from hs_add(0)

    def hs_add(t):
        for c in range(4):
            tt(A[:, c, acols(t)], A[:, c, acols(t)], A[:, c, acols(2 + t)], ALU.add,
               [("A", c, t), ("A", c, 2 + t)], [("A", c, t)])
        yield

    def sample_front_tail():
        init = lambda c: A[:, c, 2 + 3 * T:2 + 3 * T + 1]
        yield from lru_dir(1, XC, [(0, T, init)], lambda c: A[:, c, acols(2)], lambda c: [("A", c, 2)], xset=0)
        yield from hs_add(0)

    HX = 16
    XT2 = LR
    plan = [(3, XT, HX), (2, XT2, HX), (0, XT, HK), (1, XT2, HK + 8)]
    rbs = {}
    xslots = [wload(win_d[0], 2048), wload(win_d[1], 2048)]
    for i in range(2):
        t, xb, hs_ = plan[i]
        rbs[t] = xr_front(xs_d, t * T, xb, SQ + 4 * (i % 2), 4)
    for i in range(4):
        t, xb, hs_ = plan[i]
        bks = xr_back_a(xb, rbs[t], 0, hs_, xslots)
        if i + 2 < 4:
            t2, xb2, _ = plan[i + 2]
            rbs[t2] = xr_front(xs_d, t2 * T, xb2, SQ + 4 * (i % 2), 4)
        xr_back_b(bks, [t], [(0, T, 2 + t * T)])

    def _w2calc():
        for v in range(2):
            stt(w2[:, :, v], modT[:, 32:40, v], 1.0, pp[:, _off["norm2"]:_off["norm2"] + 8], ALU.add, ALU.mult,
                mod2keys(v) + ["pp"], [("w2", v)])

    def _mk_hook(i):
        def h():
            mod_part(*MOD2[i])
            if i == 3:
                _w2calc()
        return h
    mod_hooks = [_mk_hook(i) for i in range(4)]
    interleave(phase_b2(xp_d, 0, True, H2, 1, 256, yp_d, 0), sample_front(), {None: 1.0, "f": 0.0, "d": 6.0})

    def sample_back(t):
        yield from phase_b1(HK + 8 * t, lambda m: A[:, m, acols(t)], lambda m: [("A", m, t)], finish=False)
        yield from phase_b2(xs_d, t * T, True, HK + 8 * t, 0, 64, ys_d, t * T, after_v=b1_finish)

    interleave(sample_back(1), sample_front_tail(), {None: 0.0, "f": 0.0, "d": 6.0})
    drain(sample_back(0))

    if dry:
        es.close()
        return wreq
    fkeys = ["store", "store2"]
    P.finalize(nc, fkeys)
    sems = {}
    for i, k in enumerate(P.semkeys):
        sems[k] = es.enter_context(nc.semaphore("s%d" % i))
    block = es.enter_context(nc.Block())
    P.emit_all(nc, block, sems, fkeys)
    es.close()
    return nc


def _fm(vec):
    return np.ascontiguousarray(np.asarray(vec, np.float32).reshape(-1, 128).T)


def _tile8(W, ncol=256):
    K, N = W.shape
    kc = K // 128
    return np.ascontiguousarray(W.reshape(kc, 128, N // ncol, ncol).transpose(2, 1, 0, 3).reshape(N // ncol, 128, kc * ncol))


_NC_CACHE = {}


def kernel(x_prompt, x_sample, state_lru, c, c_ctx, norm1, norm2, w_ada, b_ada, w_in,
           lru_conv_w, lru_conv_b, lru_wa, lru_ba, lru_wx, lru_bx, lru_lam, sgu_ws, sgu_bs,
           g_lru, g_sgu, w_out, ffn_up, ffn_conv_w, ffn_conv_b, ffn_down, final_norm):
    f = lambda a: np.asarray(a, np.float32)
    x_prompt, x_sample, state_lru, c, c_ctx = map(f, (x_prompt, x_sample, state_lru, c, c_ctx))
    w_ada, w_in, w_out, ffn_up, ffn_down = map(f, (w_ada, w_in, w_out, ffn_up, ffn_down))
    lru_wa, lru_wx, sgu_ws, sgu_bs = map(f, (lru_wa, lru_wx, sgu_ws, sgu_bs))
    wada_t = _tile8(w_ada[0])
    win_t = _tile8(w_in[0])
    wout_t = _tile8(w_out[0])
    up = ffn_up[0]
    upp = np.concatenate([up[:, :3072].reshape(1024, 24, 128), up[:, 3072:].reshape(1024, 24, 128)], axis=2).reshape(1024, 6144)
    wup_t = _tile8(upp)
    wdn_t = np.ascontiguousarray(ffn_down[0].reshape(2, 12, 128, 8, 128).transpose(3, 0, 2, 1, 4).reshape(16, 128, 1536))
    ident = np.eye(128, dtype=np.float32)
    in_maps = []
    for i in range(8):
        b, rev = i // 2, i % 2
        d0, d1 = (1, 0) if rev else (0, 1)
        xs = x_sample[b][::-1] if rev else x_sample[b]
        xp = x_prompt[2 * i:2 * i + 2]
        if rev:
            xp = xp[:, ::-1]
        pp = np.zeros((128, NPP), np.float32)

        def put(name, arr):
            pp[:, _off[name]:_off[name] + arr.shape[1]] = arr
        put("norm1", _fm(norm1[0])); put("norm2", _fm(norm2[0])); put("fnorm", _fm(final_norm))
        put("bada", _fm(b_ada[0]))
        cw = f(lru_conv_w[0])
        z = np.zeros((1, 512), np.float32)
        w5 = np.concatenate([z, cw[::-1]], 0) if rev else np.concatenate([cw, z], 0)
        put("cw", np.concatenate([_fm(w5[t_]) for t_ in range(5)], 1))
        put("cb", _fm(lru_conv_b[0]))
        put("ba", np.concatenate([_fm(f(lru_ba)[0, d0]), _fm(f(lru_ba)[0, d1])], 1))
        put("bx", np.concatenate([_fm(f(lru_bx)[0, d0]), _fm(f(lru_bx)[0, d1])], 1))
        put("lam", np.concatenate([_fm(f(lru_lam)[0, d0]), _fm(f(lru_lam)[0, d1])], 1))
        put("h0", np.concatenate([_fm(state_lru[b, 0, d0]), _fm(state_lru[b, 0, d1])], 1))
        put("glru", _fm(g_lru[0])); put("gsgu", _fm(g_sgu[0]))
        fcw = f(ffn_conv_w[0])
        if rev:
            fcw = fcw[::-1]
        put("fw", np.concatenate([_fm(fcw[t_]) for t_ in range(3)], 1))
        put("fb", _fm(ffn_conv_b[0]))
        pp[:, _off["eps"]] = EPS
        pp[:, _off["one"]] = 1.0
        pp[:, _off["oneg"]] = np.float32(1.0) + np.float32(4e-7)
        cvec = np.zeros((128, 16), np.float32)
        cvec[:, 0::2] = _fm(c[b])
        cvec[:, 1::2] = _fm(c_ctx)
        gwm = np.zeros((128, 16, 128), np.float32)
        for dpi, do in enumerate((d0, d1)):
            for kind, wsrc in enumerate((lru_wa, lru_wx)):
                for cc in range(4):
                    idx = dpi * 8 + kind * 4 + cc
                    gwm[0:64, idx, 0:64] = wsrc[0, do, 2 * cc]
                    gwm[64:128, idx, 64:128] = wsrc[0, do, 2 * cc + 1]
        ws = sgu_ws[0]
        bs = sgu_bs[0]
        if rev:
            ws = ws[:, ::-1, ::-1]
            bs = bs[:, ::-1]
        wsT = np.ascontiguousarray(ws.transpose(2, 0, 1)).reshape(128, 512)
        in_maps.append({
            "xs": np.ascontiguousarray(xs.T), "xp": np.ascontiguousarray(xp.reshape(512, 1024).T),
            "pp": pp, "cvec": cvec, "bsrow": np.ascontiguousarray(bs.reshape(1, 512)),
            "gw": gwm.reshape(128, 2048), "wsT": wsT,
            "wada": wada_t, "win": win_t, "wout": wout_t, "wup": wup_t, "wdn": wdn_t,
        })
    if "nc" not in _NC_CACHE:
        _NC_CACHE["nc"] = build_program()
    nc = _NC_CACHE["nc"]
    res = run_bass_kernel_spmd(nc, in_maps, core_ids=list(range(8)))
    y_prompt = np.zeros((16, 256, 1024), np.float32)
    y_sample = np.zeros((4, 2048, 1024), np.float32)
    new_state = np.zeros((16, 1, 2, 512), np.float32)
    for i in range(8):
        b, rev = i // 2, i % 2
        r = res.results[i]
        ys = np.asarray(r["ys"]).T
        yp = np.asarray(r["yp"]).T.reshape(2, 256, 1024)
        nstv = np.asarray(r["nst"]).reshape(128, 2, 2, 4).transpose(1, 2, 3, 0).reshape(2, 2, 512)
        if rev:
            y_sample[b, 1024:] = ys[::-1]
            y_prompt[2 * i:2 * i + 2] = yp[:, ::-1]
            new_state[2 * i:2 * i + 2, 0, 1] = nstv[:, 0]
            new_state[2 * i:2 * i + 2, 0, 0] = nstv[:, 1]
        else:
            y_sample[b, :1024] = ys
            y_prompt[2 * i:2 * i + 2] = yp
            new_state[2 * i:2 * i + 2, 0, 0] = nstv[:, 0]
            new_state[2 * i:2 * i + 2, 0, 1] = nstv[:, 1]
    return (y_prompt, y_sample, new_state)
```

```python
import numpy as np
from contextlib import ExitStack
import concourse.bass as bass
import concourse.mybir as mybir
from concourse.bass_utils import run_bass_kernel_spmd

F32 = mybir.dt.float32
BF16 = mybir.dt.bfloat16
AF = mybir.ActivationFunctionType
ALU = mybir.AluOpType

D = 1024
T = 512
NSLOT = 6
EPS = 1e-6

_off = {}
_n = 0
for _name, _w in [("norm1", 8), ("norm2", 8), ("fnorm", 8), ("bada", 48), ("cw", 20), ("cb", 4),
                  ("ba", 8), ("bx", 8), ("lam", 8), ("h0", 8), ("glru", 4), ("gsgu", 4),
                  ("fw", 144), ("fb", 48), ("eps", 1), ("one", 1), ("oneg", 1)]:
    _off[_name] = _n
    _n += _w
NPP = _n


class Op:
    __slots__ = ("eng", "emit", "deps", "dma", "signal", "count", "waits", "idx")


class Prog:
    def __init__(self):
        self.ops = []
        self.lastw = {}
        self.readers = {}

    def op(self, eng, emit, reads=(), writes=(), dma=None):
        o = Op()
        o.eng, o.emit, o.dma, o.signal, o.count, o.waits = eng, emit, dma, False, 0, []
        o.idx = len(self.ops)
        deps = {}
        for r in reads:
            w = self.lastw.get(r)
            if w is not None:
                deps[w] = "raw"
        for w_ in writes:
            lw = self.lastw.get(w_)
            if lw is not None and lw not in deps:
                deps[lw] = "waw"
            for rd in self.readers.get(w_, ()):
                if rd not in deps:
                    deps[rd] = "war"
        for r in reads:
            self.readers.setdefault(r, []).append(o.idx)
        for w_ in writes:
            self.lastw[w_] = o.idx
            self.readers[w_] = []
        o.deps = deps
        self.ops.append(o)
        return o

    def finalize(self, nc, final_wait_keys):
        ops = self.ops
        for o in ops:
            for d, kind in o.deps.items():
                do = ops[d]
                if do.dma is None and o.dma is None and do.eng == o.eng:
                    if o.eng == "pe":
                        continue
                do.signal = True
                o.waits.append(d)
        for o in ops:
            if o.dma is not None and o.dma in final_wait_keys:
                o.signal = True
        cnt = {}
        for o in ops:
            if not o.signal:
                continue
            key = ("dma", o.dma) if o.dma is not None else ("eng", o.eng)
            cnt[key] = cnt.get(key, 0) + (16 if o.dma is not None else 1)
            o.count = cnt[key]
        self.totals = cnt
        self.semkeys = list(cnt.keys())

    def emit_all(self, nc, block, sems, final_wait_keys):
        engs = {"pe": "tensor", "act": "scalar", "dve": "vector", "pool": "gpsimd", "sp": "sync"}
        ops = self.ops
        totals = self.totals

        def target(d):
            do = ops[d]
            if do.dma is not None:
                key = ("dma", do.dma)
                val = totals[key] if isinstance(do.dma, str) and do.dma.startswith("const") else do.count
                return key, val
            return ("eng", do.eng), do.count

        for ename, bname in engs.items():
            mine = [o for o in ops if o.eng == ename]

            def body(e, mine=mine, ename=ename):
                waited = {}
                for o in mine:
                    tg = {}
                    for d in o.waits:
                        k, v = target(d)
                        if v > tg.get(k, 0):
                            tg[k] = v
                    for k, v in tg.items():
                        if waited.get(k, 0) < v:
                            e.wait_ge(sems[k], v)
                            waited[k] = v
                    ins = o.emit(e)
                    if o.signal:
                        key = ("dma", o.dma) if o.dma is not None else ("eng", o.eng)
                        ins.then_inc(sems[key], 16 if o.dma is not None else 1)
                if ename == "sp":
                    for k in final_wait_keys:
                        kk = ("dma", k)
                        if kk in totals:
                            e.wait_ge(sems[kk], totals[kk])

            getattr(block, bname)(body)


def build_program():
    ws = _build(None)
    return _build(ws)


def _build(wsched):
    nc = bass.Bass("TRN2", target_bir_lowering=False)
    P = Prog()
    dry = wsched is None
    wreq = []

    def dram(name, shape, kind="ExternalInput", dt=F32):
        return nc.dram_tensor(name, list(shape), dt, kind=kind).ap()

    xs_d = dram("xs", [D, 2048])
    xp_d = dram("xp", [D, 512])
    pp_d = dram("pp", [128, NPP])
    cvec_d = dram("cvec", [128, 16])
    bsrow_d = dram("bsrow", [1, 512])
    gw_d = dram("gw", [128, 16 * 128])
    ws_d = dram("wsT", [128, 4 * 128])
    wada_d = dram("wada", [24, 128, 2048])
    win_d = dram("win", [8, 128, 2048])
    wout_d = dram("wout", [4, 128, 2048])
    wup_d = dram("wup", [24, 128, 2048])
    wdn_d = dram("wdn", [16, 128, 1536])
    ys_d = dram("ys", [D, 1024], kind="ExternalOutput")
    yp_d = dram("yp", [D, 512], kind="ExternalOutput")
    nst_d = dram("nst", [128, 16], kind="ExternalOutput")

    es = ExitStack()

    def sb(name, shape, dt=F32):
        return es.enter_context(nc.sbuf_tensor("sb_" + name, list(shape), dt))

    pp = sb("pp", [128, NPP])
    cvec = sb("cvec", [128, 16])
    csil = sb("csil", [128, 16], BF16)
    ones_bf = sb("ones_bf", [128, 128], BF16)
    ones_row = sb("ones_row", [1, 128])
    bsrow = sb("bsrow", [1, 512])
    gw = sb("gw", [128, 16, 128], BF16)
    wsT = sb("wsT", [128, 4, 128], BF16)
    bsbc = sb("bsbc", [128, 4, 128])
    modT = sb("modT", [128, 48, 2])
    w1 = sb("w1", [128, 8, 2])
    w2 = sb("w2", [128, 8, 2])
    lamc = sb("lamc", [128, 8])
    lamh = sb("lamh", [128, 8])
    hba = sb("hba", [128, 8])
    hbx = sb("hbx", [128, 8])
    tmpl = sb("tmpl", [128, 8])
    carry = sb("carry", [128, 8])
    nst = sb("nst", [128, 16])
    wring = sb("wring", [128, NSLOT, 2048], BF16)
    NS = 41
    S = sb("S", [128, NS, T])
    NH = 56
    H = sb("H", [128, NH, T], BF16)
    A = sb("A", [128, 4, 2052])
    ps = es.enter_context(nc.psum_tensor("psum_all", [128, 8, T], F32))

    XT = 0
    GU, OS, NT, SR = 8, 12, 16, 18
    FT = 8
    LR = 19
    AS, A2, TI, TR = LR, LR + 4, LR + 8, LR + 12
    GY, OL = LR, LR + 2
    XC = 33
    HID, SQ, OO, GV = 0, 0, 8, 16
    H2 = 24
    HK = 32
    XCB = 48

    def pcol(name, i=0):
        c = _off[name] + i
        return pp[:, c:c + 1]

    state = {"bank": 0, "sbank": 0, "wslot": 0, "wissued": 0}

    def bank():
        b = state["bank"]
        state["bank"] = (b + 1) % 6
        return b

    def sbank():
        b = 6 + state["sbank"]
        state["sbank"] = 1 - state["sbank"]
        return b

    def mm(out, lhsT, rhs, start, stop, reads, writes):
        P.op("pe", lambda e: e.matmul(out, lhsT=lhsT, rhs=rhs, start=start, stop=stop), reads, writes)

    def act(out, in_, func, reads, writes, bias=None, scale=None):
        kw = {}
        if bias is not None:
            kw["bias"] = bias
        if scale is not None:
            kw["scale"] = scale
        P.op("act", lambda e: e.activation(out=out, in_=in_, func=func, **kw), reads, writes)

    def stt(out, in0, scalar, in1, op0, op1, reads, writes):
        P.op("dve", lambda e: e.scalar_tensor_tensor(out=out, in0=in0, scalar=scalar, in1=in1, op0=op0, op1=op1),
             reads, writes)

    def tt(out, in0, in1, op, reads, writes):
        P.op("dve", lambda e: e.tensor_tensor(out=out, in0=in0, in1=in1, op=op), reads, writes)

    def ts(out, in0, s1, s2, op0, op1, reads, writes):
        if s2 is None:
            P.op("dve", lambda e: e.tensor_scalar(out=out, in0=in0, scalar1=s1, scalar2=None, op0=op0), reads, writes)
        else:
            P.op("dve", lambda e: e.tensor_scalar(out=out, in0=in0, scalar1=s1, scalar2=s2, op0=op0, op1=op1),
                 reads, writes)

    def ptt(out, in0, in1, op, reads, writes):
        tt(out, in0, in1, op, reads, writes)

    def psq(out, in_, reads, writes):
        act(out, in_, AF.Square, reads, writes)

    def copy_col(dst, src, reads, writes):
        P.op("dve", lambda e: e.tensor_copy(out=dst, in_=src), reads, writes)

    def _wissue(u, src_ap, nel):
        s = u % NSLOT
        P.op("pool", lambda e: e.dma_start(out=wring[:, s, 0:nel], in_=src_ap), (), [("w", s)], dma=("w", s))

    def wload(src_ap, nel):
        i = state["wslot"]
        state["wslot"] = i + 1
        if dry:
            wreq.append((src_ap, nel))
            _wissue(i, src_ap, nel)
            return i % NSLOT
        while state["wissued"] < min(len(wsched), i + NSLOT - 1):
            u = state["wissued"]
            _wissue(u, *wsched[u])
            state["wissued"] = u + 1
        return i % NSLOT

    def drain(g):
        for _ in g:
            pass

    def interleave(main, side, rate):
        acc = 0.0
        ds = False
        for tag in main:
            acc += rate.get(tag, rate[None])
            while acc >= 1.0 and not ds:
                acc -= 1.0
                try:
                    next(side)
                except StopIteration:
                    ds = True
        for _ in side:
            pass

    P.op("sp", lambda e: e.dma_start(out=pp[:], in_=pp_d), (), ["pp"], dma="const")
    P.op("sp", lambda e: e.dma_start(out=cvec[:], in_=cvec_d), (), ["cvec"], dma="const")
    P.op("sp", lambda e: e.dma_start(out=bsrow[:], in_=bsrow_d), (), ["bsrow"], dma="const")
    P.op("pool", lambda e: e.dma_start(out=gw[:].rearrange("p a b -> p (a b)"), in_=gw_d), (), ["gw"], dma="constb")
    P.op("pool", lambda e: e.dma_start(out=wsT[:].rearrange("p a b -> p (a b)"), in_=ws_d), (), ["wsT"], dma="constb")
    P.op("dve", lambda e: e.memset(ones_bf[:], 1.0), (), ["ones_bf"])
    P.op("dve", lambda e: e.memset(ones_row[:], 1.0), (), ["ones_row"])
    allA = [("A", c, t) for c in range(4) for t in range(4)]
    P.op("dve", lambda e: e.memset(A[:], 0.0), (), allA + ["Apad"])
    P.op("dve", lambda e: e.memset(carry[:], 0.0), (), ["carry"])

    act(csil[:], cvec[:], AF.Silu, ["cvec"], ["csil"])
    lam_ap = pp[:, _off["lam"]:_off["lam"] + 8]
    act(tmpl[:], lam_ap, AF.Exp, ["pp"], ["tmpl"], scale=-1.0)
    act(tmpl[:], tmpl[:], AF.Ln, ["tmpl"], ["tmpl"], bias=pcol("one"))
    ts(lamc[:], tmpl[:], -8.0, None, ALU.mult, None, ["tmpl"], ["lamc"])
    ts(lamh[:], tmpl[:], -4.0, None, ALU.mult, None, ["tmpl"], ["lamh"])
    ts(hba[:], pp[:, _off["ba"]:_off["ba"] + 8], 0.5, None, ALU.mult, None, ["pp"], ["hba"])
    ts(hbx[:], pp[:, _off["bx"]:_off["bx"] + 8], 0.5, None, ALU.mult, None, ["pp"], ["hbx"])

    _bb = bank()
    for g in range(4):
        mm(ps[:, _bb, g * 128:g * 128 + 128], ones_row[0:1, :], bsrow[0:1, g * 128:g * 128 + 128], True, True,
           ["ones_row", "bsrow"], [("ps", _bb)])
    act(bsbc[:].rearrange("p g q -> p (g q)"), ps[:, _bb, :], AF.Copy, [("ps", _bb)], ["bsbc"])


    def mod_part(u0, u1):
        MODB = bank()
        for u in range(u0, u1):
            s = wload(wada_d[u], 2048)
            for mmi in range(2):
                j = u * 2 + mmi
                for k in range(8):
                    mm(ps[:, MODB, 2 * j:2 * j + 2], wring[:, s, k * 256 + mmi * 128:k * 256 + mmi * 128 + 128],
                       csil[:, 2 * k:2 * k + 2], k == 0, k == 7, [("w", s), "csil"], [("ps", MODB)])
        j0, j1 = u0 * 2, u1 * 2
        for v in range(2):
            tt(modT[:, j0:j1, v], ps[:, MODB, 0:96].rearrange("p (j v) -> p j v", v=2)[:, j0:j1, v],
               pp[:, _off["bada"] + j0:_off["bada"] + j1], ALU.add, [("ps", MODB), "pp"], [("mod", v, j0)])

    MOD2 = [(8, 12), (12, 16), (16, 20), (20, 24)]

    def mod2keys(v):
        return [("mod", v, 2 * u0) for (u0, u1) in MOD2]

    def mod1():
        mod_part(0, 8)
        for v in range(2):
            stt(w1[:, :, v], modT[:, 8:16, v], 1.0, pp[:, _off["norm1"]:_off["norm1"] + 8], ALU.add, ALU.mult,
                [("mod", v, 0), "pp"], [("w1", v)])

    def load_x(src_d, col0, xb=None):
        xb = XT if xb is None else xb
        P.op("sp", lambda e: e.dma_start(out=S[:, xb:xb + 8, :],
                                         in_=src_d.rearrange("(k p) t -> p k t", p=128)[:, :, col0:col0 + T]),
             (), [("S", xb + k) for k in range(8)], dma=("xload", xb))

    def rstd_bank(sq0, nk, dim):
        b0 = sbank()
        for k in range(nk):
            mm(ps[:, b0, :], ones_bf[:], H[:, sq0 + k, :], k == 0, k == nk - 1, ["ones_bf", ("H", sq0 + k)], [("ps", b0)])
        act(ps[:, b0, :], ps[:, b0, :], AF.Ln, [("ps", b0), "pp"], [("ps", b0)], bias=pcol("eps"), scale=1.0 / dim)
        act(ps[:, b0, :], ps[:, b0, :], AF.Exp, [("ps", b0)], [("ps", b0)], scale=-0.5)
        return b0

    def norm_front(xb, sq0=None, nrot=8):
        sq0 = SQ if sq0 is None else sq0
        b0 = sbank()
        for k in range(8):
            sl = sq0 + (k % nrot)
            psq(H[:, sl, :], S[:, xb + k, :], [("S", xb + k)], [("H", sl)])
            mm(ps[:, b0, :], ones_bf[:], H[:, sl, :], k == 0, k == 7, ["ones_bf", ("H", sl)], [("ps", b0)])
        act(ps[:, b0, :], ps[:, b0, :], AF.Ln, [("ps", b0), "pp"], [("ps", b0)], bias=pcol("eps"), scale=1.0 / float(D))
        act(ps[:, b0, :], ps[:, b0, :], AF.Exp, [("ps", b0)], [("ps", b0)], scale=-0.5)
        return b0

    def norm_back(xb, rb, wv, shv, v, hslot, modkeys, dve_bias=False):
        for k in range(8):
            tmp = NT + (k % 2)
            stt(S[:, tmp, :], S[:, xb + k, :], wv[:, k, v:v + 1], ps[:, rb, :], ALU.mult, ALU.mult,
                [("S", xb + k), ("ps", rb)] + modkeys, [("S", tmp)])
            if dve_bias:
                ts(H[:, hslot + k, :], S[:, tmp, :], shv[:, k, v:v + 1], None, ALU.add, None,
                   [("S", tmp)] + modkeys, [("H", hslot + k)])
            else:
                act(H[:, hslot + k, :], S[:, tmp, :], AF.Identity, [("S", tmp)] + modkeys, [("H", hslot + k)],
                    bias=shv[:, k, v:v + 1])

    def norm_mod(wv, shv, v, hslot, modkeys):
        rb = norm_front(XT)
        norm_back(XT, rb, wv, shv, v, hslot, modkeys)

    def proj_in(unit0, hslot, m, evac):
        s = unit0[m // 2]
        b = bank()
        for k in range(8):
            mm(ps[:, b, :], wring[:, s, k * 256 + (m % 2) * 128:k * 256 + (m % 2) * 128 + 128], H[:, hslot + k, :],
               k == 0, k == 7, [("w", s), ("H", hslot + k)], [("ps", b)])
        evac(b)

    def xr_front(src_d, col0, xb, sq0=None, nrot=8):
        load_x(src_d, col0, xb)
        return norm_front(xb, sq0, nrot)

    def xr_back_a(xb, rb, v, hslot, slots=None):
        norm_back(xb, rb, w1, modT[:, 0:8, :], v, hslot, [("w1", v), ("mod", v, 0)], dve_bias=True)
        lazy = slots is None
        if lazy:
            slots = [None, None]
        if lazy:
            slots = [wload(win_d[0], 2048), wload(win_d[1], 2048)]
        banks = [bank() for _ in range(4)]
        for k in range(8):
            for m in range(4):
                s_ = slots[m // 2]
                mm(ps[:, banks[m], :], wring[:, s_, k * 256 + (m % 2) * 128:k * 256 + (m % 2) * 128 + 128],
                   H[:, hslot + k, :], k == 0, k == 7, [("w", s_), ("H", hslot + k)], [("ps", banks[m])])
        return banks

    def xr_back_b(banks, akeys, seqs):
        for m, b in enumerate(banks):
            for (tc, ln, ac) in seqs:
                act(A[:, m, ac:ac + ln], ps[:, b, tc:tc + ln], AF.Copy, [("ps", b)], [("A", m, t_) for t_ in akeys])

    def xr_back(xb, rb, v, hslot, akeys, seqs, slots=None):
        xr_back_b(xr_back_a(xb, rb, v, hslot, slots), akeys, seqs)

    def xr_pass(src_d, col0, v, hslot, akeys, seqs):
        rb = xr_front(src_d, col0, XT)
        xr_back(XT, rb, v, hslot, akeys, seqs)

    def conv_tile(akeys, segs, xc_slot, xset=0):
        two = len(segs) == 2
        if two:
            (_, ln0, ac0), (_, _, ac1) = segs
            pitch = ac1 - ac0
        def src(c, tap):
            if two:
                return A[:, c, ac0 - 2 + tap:ac0 - 2 + tap + 2 * pitch].rearrange("p (s l) -> p s l", l=pitch)[:, :, 0:ln0]
            (tc, ln, ac) = segs[0]
            return A[:, c, ac - 2 + tap:ac - 2 + tap + ln]

        def dstv(c):
            return S[:, xc_slot + c, :].rearrange("p (s l) -> p s l", l=ln0) if two else S[:, xc_slot + c, :]

        for c0 in (0, 2):
            cs = (c0, c0 + 1)
            for c in cs:
                rk = ["pp", "Apad"] + [("A", c, t) for t in akeys]
                ts(dstv(c), src(c, 2), pcol("cw", 2 * 4 + c), pcol("cb", c), ALU.mult, ALU.add, rk, [("S", xc_slot + c)])
            for tap in (0, 1, 3, 4):
                for c in cs:
                    rk = ["pp", "Apad"] + [("A", c, t) for t in akeys]
                    stt(dstv(c), src(c, tap), pcol("cw", tap * 4 + c), dstv(c), ALU.mult, ALU.add,
                        rk + [("S", xc_slot + c)], [("S", xc_slot + c)])
            for c in cs:
                xb_ = XCB + 4 * xset + c
                P.op("dve", lambda e, c=c, xb_=xb_: e.tensor_copy(out=H[:, xb_, :], in_=S[:, xc_slot + c, :]),
                     [("S", xc_slot + c)], [("H", xb_)])
                yield

    lru_state = {"set": 0}

    def lru_dir(d, xc_slot, segs_scan, out_fn, out_keys_fn, post_fn=None, tmp_out=False, xset=0):
        st0 = lru_state["set"]
        bases = [LR + 7 * st0, LR + 7 * (1 - st0)]
        for half in range(2):
            base = bases[half]
            AS_, A2_, TI_, TR_ = base, base + 2, base + 4, base + 6
            cs = (2 * half, 2 * half + 1)
            for i, c in enumerate(cs):
                xb_ = XCB + 4 * xset + c
                br, bi = bank(), bank()
                mm(ps[:, br, :], gw[:, d * 8 + c, :], H[:, xb_, :], True, True, ["gw", ("H", xb_)], [("ps", br)])
                mm(ps[:, bi, :], gw[:, d * 8 + 4 + c, :], H[:, xb_, :], True, True, ["gw", ("H", xb_)], [("ps", bi)])
                dc = d * 4 + c
                act(S[:, TR_, :], ps[:, br, :], AF.Tanh, [("ps", br), "hba"], [("S", TR_)], bias=hba[:, dc:dc + 1], scale=0.5)
                act(S[:, TI_ + i, :], ps[:, bi, :], AF.Tanh, [("ps", bi), "hbx"], [("S", TI_ + i)], bias=hbx[:, dc:dc + 1], scale=0.5)
                act(S[:, AS_ + i, :], S[:, TR_, :], AF.Exp, [("S", TR_), "lamh"], [("S", AS_ + i)],
                    bias=lamh[:, dc:dc + 1], scale=lamh[:, dc:dc + 1])
                act(S[:, A2_ + i, :], S[:, TR_, :], AF.Exp, [("S", TR_), "lamc"], [("S", A2_ + i)],
                    bias=lamc[:, dc:dc + 1], scale=lamc[:, dc:dc + 1])
            yield
        for half in range(2):
            A2_ = bases[half] + 2
            for i in range(2):
                act(S[:, A2_ + i, :], S[:, A2_ + i, :], AF.Sqrt, [("S", A2_ + i), "pp"], [("S", A2_ + i)],
                    bias=pcol("oneg"), scale=-1.0)
        yield
        for half in range(2):
            base = bases[half]
            AS_, A2_, TI_, TR_ = base, base + 2, base + 4, base + 6
            cs = (2 * half, 2 * half + 1)
            for i, c in enumerate(cs):
                stt(S[:, TI_ + i, :], S[:, TI_ + i, :], 1.0, S[:, xc_slot + c, :], ALU.add, ALU.mult,
                    [("S", TI_ + i), ("S", xc_slot + c)], [("S", TI_ + i)])
            for i, c in enumerate(cs):
                stt(S[:, TI_ + i, :], S[:, A2_ + i, :], 0.5, S[:, TI_ + i, :], ALU.mult, ALU.mult,
                    [("S", TI_ + i), ("S", A2_ + i)], [("S", TI_ + i)])
            for i, c in enumerate(cs):
                if tmp_out:
                    dst = S[:, A2_ + i, :]
                    okeys = [("S", A2_ + i)]
                else:
                    dst = out_fn(c)
                    okeys = out_keys_fn(c)
                for (tc, ln, init) in segs_scan:
                    ini = init(c) if callable(init) else init
                    rd = [("S", AS_ + i), ("S", TI_ + i), "carry", "pp"] + [("A", c, t) for t in range(4)]
                    if d == 0:
                        P.op("dve", lambda e, dst=dst, a_=AS_ + i, b_=TI_ + i, tc=tc, ln=ln, ini=ini: e.tensor_tensor_scan(
                            out=dst[:, tc:tc + ln], data0=S[:, a_, tc:tc + ln], data1=S[:, b_, tc:tc + ln],
                            initial=ini, op0=ALU.mult, op1=ALU.add), rd, okeys)
                    else:
                        P.op("dve", lambda e, dst=dst, a_=AS_ + i, b_=TI_ + i, tc=tc, ln=ln, ini=ini: e.tensor_tensor_scan(
                            out=dst[:, tc:tc + ln][:, ::-1], data0=S[:, a_, tc:tc + ln][:, ::-1],
                            data1=S[:, b_, tc:tc + ln][:, ::-1],
                            initial=ini, op0=ALU.mult, op1=ALU.add), rd, okeys)
                if post_fn is not None:
                    post_fn(c, dst, okeys)
            yield

    def phase_b1(hslot, hs_fn, hs_keys_fn, finish=True):
        bks = []
        for m in range(4):
            if m % 2 == 0:
                s_ = wload(win_d[2 + m // 2], 2048)
            b = bank()
            for k in range(8):
                mm(ps[:, b, :], wring[:, s_, k * 256 + (m % 2) * 128:k * 256 + (m % 2) * 128 + 128], H[:, hslot + k, :],
                   k == 0, k == 7, [("w", s_), ("H", hslot + k)], [("ps", b)])
            bks.append(b)
        GY4 = GY
        for m, b in enumerate(bks):
            act(S[:, OL + m, :], ps[:, b, :], AF.Gelu_apprx_tanh, [("ps", b)], [("S", OL + m)])
        for m in range(4):
            tt(S[:, OL + m, :], S[:, OL + m, :], hs_fn(m), ALU.mult, [("S", OL + m)] + hs_keys_fn(m), [("S", OL + m)])
        for m in range(4):
            psq(H[:, SQ + m, :], S[:, OL + m, :], [("S", OL + m)], [("H", SQ + m)])
        yield
        if finish:
            b1_finish()
        yield

    def b1_finish():
        rb_l = rstd_bank(SQ, 4, 512.0)
        for m in range(4):
            stt(H[:, OO + m, :], S[:, OL + m, :], pcol("glru", m), ps[:, rb_l, :], ALU.mult, ALU.mult,
                [("S", OL + m), ("ps", rb_l), "pp"], [("H", OO + m)])

    def phase_b2(x_src, x_col0, reload_x, hslot, v, rowlen, y_d, y_col0, hooks=(), after_v=None):
        hooks = list(hooks)
        g1 = modT[:, 16:24, :]
        sh2 = modT[:, 24:32, :]
        g2 = modT[:, 40:48, :]
        mk = mod2keys(v)
        if reload_x:
            load_x(x_src, x_col0)
        bks = []
        for m in range(4):
            if m % 2 == 0:
                s_ = wload(win_d[4 + m // 2], 2048)
            b = bank()
            for k in range(8):
                mm(ps[:, b, :], wring[:, s_, k * 256 + (m % 2) * 128:k * 256 + (m % 2) * 128 + 128], H[:, hslot + k, :],
                   k == 0, k == 7, [("w", s_), ("H", hslot + k)], [("ps", b)])
            bks.append(b)
        for m, b in enumerate(bks):
            act(S[:, GU + m, :], ps[:, b, :], AF.Gelu_apprx_tanh, [("ps", b)], [("S", GU + m)])
        yield
        slots = [wload(win_d[6], 2048), wload(win_d[7], 2048)]
        bks = []
        for n in range(4):
            b = bank()
            for hf_ in range(2):
                s = slots[hf_]
                for k in range(8):
                    mm(ps[:, b, hf_ * 256:hf_ * 256 + 256], H[:, hslot + k, n * 128:n * 128 + 128],
                       wring[:, s, k * 256:k * 256 + 256], k == 0, k == 7, [("w", s), ("H", hslot + k)], [("ps", b)])
            bks.append(b)
        for n, b in enumerate(bks):
            act(H[:, GV + n, :], ps[:, b, :], AF.Gelu_apprx_tanh, [("ps", b)], [("H", GV + n)])
        yield
        sb_ = [bank() for _ in range(4)]
        for g in range(4):
            for n in range(4):
                mm(ps[:, sb_[g], n * 128:n * 128 + 128], H[:, GV + n, g * 128:g * 128 + 128], wsT[:, g, :], True, True,
                   [("H", GV + n), "wsT"], [("ps", sb_[g])])
        for g in range(4):
            tt(S[:, OS + g, :].rearrange("p (n q) -> p n q", q=128), ps[:, sb_[g], :].rearrange("p (n q) -> p n q", q=128),
               bsbc[:, g, :].unsqueeze(1).to_broadcast([128, 4, 128]), ALU.add, [("ps", sb_[g]), "bsbc"], [("S", OS + g)])
        for g in range(4):
            tt(S[:, OS + g, :], S[:, OS + g, :], S[:, GU + g, :], ALU.mult, [("S", OS + g), ("S", GU + g)], [("S", OS + g)])
        for g in range(4):
            psq(H[:, SQ + 4 + g, :], S[:, OS + g, :], [("S", OS + g)], [("H", SQ + 4 + g)])
        if after_v is not None:
            after_v()
        rb_s = rstd_bank(SQ + 4, 4, 512.0)
        for m in range(4):
            stt(H[:, OO + 4 + m, :], S[:, OS + m, :], pcol("gsgu", m), ps[:, rb_s, :], ALU.mult, ALU.mult,
                [("S", OS + m), ("ps", rb_s), "pp"], [("H", OO + 4 + m)])
        yield
        for m in range(8):
            if m % 2 == 0:
                s = wload(wout_d[m // 2], 2048)
            b = bank()
            for k in range(8):
                mm(ps[:, b, :], wring[:, s, k * 256 + (m % 2) * 128:k * 256 + (m % 2) * 128 + 128], H[:, OO + k, :],
                   k == 0, k == 7, [("w", s), ("H", OO + k)], [("ps", b)])
            stt(S[:, XT + m, :], ps[:, b, :], g1[:, m, v:v + 1], S[:, XT + m, :], ALU.mult, ALU.add,
                [("ps", b), ("S", XT + m)] + mk, [("S", XT + m)])
            yield
        norm_mod(w2, sh2, v, H2, [("w2", v)] + mk)
        yield
        pend = None
        for j in range(24):
            s = wload(wup_d[j], 2048)
            bg, bv = bank(), bank()
            for k in range(8):
                mm(ps[:, bg, :], wring[:, s, k * 256:k * 256 + 128], H[:, H2 + k, :], k == 0, k == 7,
                   [("w", s), ("H", H2 + k)], [("ps", bg)])
            for k in range(8):
                mm(ps[:, bv, :], wring[:, s, k * 256 + 128:k * 256 + 256], H[:, H2 + k, :], k == 0, k == 7,
                   [("w", s), ("H", H2 + k)], [("ps", bv)])
            o3 = (j % 2) * 3
            for (bk, fc, tslot) in ((bg, j, FT + o3), (bv, 24 + j, FT + o3 + 1)):
                act(S[:, tslot, :], ps[:, bk, :], AF.Identity, [("ps", bk), "pp"], [("S", tslot)],
                    bias=pcol("fb", fc), scale=pcol("fw", 48 + fc))
                pv = ps[:, bk, :].rearrange("p (r l) -> p r l", l=rowlen)
                sv = S[:, tslot, :].rearrange("p (r l) -> p r l", l=rowlen)
                stt(sv[:, :, 1:rowlen], pv[:, :, 0:rowlen - 1], pcol("fw", fc), sv[:, :, 1:rowlen], ALU.mult, ALU.add,
                    [("ps", bk), ("S", tslot), "pp"], [("S", tslot)])
                stt(sv[:, :, 0:rowlen - 1], pv[:, :, 1:rowlen], pcol("fw", 96 + fc), sv[:, :, 0:rowlen - 1], ALU.mult, ALU.add,
                    [("ps", bk), ("S", tslot), "pp"], [("S", tslot)])
            if pend is not None:
                pend()
            sg = FT + o3 + 2
            act(S[:, sg, :], S[:, FT + o3, :], AF.Silu, [("S", FT + o3)], [("S", sg)])

            def pend(j=j, sg=sg, o3=o3):
                ptt(H[:, HID + j, :], S[:, sg, :], S[:, FT + o3 + 1, :], ALU.mult, [("S", sg), ("S", FT + o3 + 1)],
                    [("H", HID + j)])
            yield "f"
        pend()
        for m in range(8):
            s0 = wload(wdn_d[2 * m], 1536)
            s1 = wload(wdn_d[2 * m + 1], 1536)
            b = bank()
            for k in range(24):
                s = s0 if k < 12 else s1
                kk = k % 12
                mm(ps[:, b, :], wring[:, s, kk * 128:kk * 128 + 128], H[:, HID + k, :], k == 0, k == 23,
                   [("w", s), ("H", HID + k)], [("ps", b)])
            stt(S[:, XT + m, :], ps[:, b, :], g2[:, m, v:v + 1], S[:, XT + m, :], ALU.mult, ALU.add,
                [("ps", b), ("S", XT + m)] + mk, [("S", XT + m)])
            yield "d"
        for k in range(8):
            psq(H[:, SQ + k, :], S[:, XT + k, :], [("S", XT + k)], [("H", SQ + k)])
        rb = rstd_bank(SQ, 8, float(D))
        for k in range(8):
            stt(S[:, XT + k, :], S[:, XT + k, :], pcol("fnorm", k), ps[:, rb, :], ALU.mult, ALU.mult,
                [("S", XT + k), ("ps", rb), "pp"], [("S", XT + k)])
        P.op("sp", lambda e: e.dma_start(out=y_d.rearrange("(k p) t -> p k t", p=128)[:, :, y_col0:y_col0 + T],
                                         in_=S[:, XT:XT + 8, :]),
             [("S", XT + k) for k in range(8)], [], dma="store")
        yield

    def h0col(d):
        return lambda c: pp[:, _off["h0"] + d * 4 + c:_off["h0"] + d * 4 + c + 1]

    def carrycol(d):
        return lambda c: carry[:, d * 4 + c:d * 4 + c + 1]

    def acols(t):
        return slice(2 + t * T, 2 + (t + 1) * T)

    P.op("dve", lambda e: e.memset(A[:, :, 258:262], 0.0), (), allA)
    P.op("dve", lambda e: e.memset(A[:, :, 518:520], 0.0), (), allA)
    pseg = [(0, 256, 2), (256, 256, 262)]
    rb_p = xr_front(xp_d, 0, XT)
    mod1()
    xr_back(XT, rb_p, 1, H2, [0, 1], pseg)
    drain(conv_tile([0, 1], pseg, XC))
    mod_part(*MOD2[0])
    drain(lru_dir(0, XC, [(0, 256, 0.0), (256, 256, 0.0)], lambda c: A[:, c, acols(2)], lambda c: [("A", c, 2)]))
    mod_part(*MOD2[1])
    drain(lru_dir(1, XC, [(0, 256, 0.0), (256, 256, 0.0)], lambda c: A[:, c, acols(3)], lambda c: [("A", c, 3)]))
    mod_part(*MOD2[2])
    for sq_ in range(2):
        for c in range(4):
            cf = 2 + 2 * T + sq_ * 256 + 255
            cb_ = 2 + 3 * T + sq_ * 256
            copy_col(nst[:, sq_ * 8 + c:sq_ * 8 + c + 1], A[:, c, cf:cf + 1], [("A", c, 2)], ["nst"])
            copy_col(nst[:, sq_ * 8 + 4 + c:sq_ * 8 + 4 + c + 1], A[:, c, cb_:cb_ + 1], [("A", c, 3)], ["nst"])
    P.op("sp", lambda e: e.dma_start(out=nst_d, in_=nst[:]), ["nst"], [], dma="store2")
    for c in range(4):
        tt(A[:, c, acols(2)], A[:, c, acols(2)], A[:, c, acols(3)], ALU.add, [("A", c, 2), ("A", c, 3)], [("A", c, 2)])
    drain(phase_b1(H2, lambda m: A[:, m, acols(2)], lambda m: [("A", m, 2)]))
    mod_part(*MOD2[3])
    for v in range(2):
        stt(w2[:, :, v], modT[:, 32:40, v], 1.0, pp[:, _off["norm2"]:_off["norm2"] + 8], ALU.add, ALU.mult,
            mod2keys(v) + ["pp"], [("w2", v)])

    def merge(*gens):
        gens = list(gens)
        while gens:
            for g in list(gens):
                try:
                    next(g)
                    yield
                except StopIteration:
                    gens.remove(g)

    def sample_front(last=False):

        def convg(t, slot, xs):
            return conv_tile([max(t - 1, 0), t, min(t + 1, 3)], [(0, T, 2 + t * T)], slot, xs)

        def bwd_other(t, slot, xs):
            init = h0col(1) if t == 3 else carrycol(1)

            def post(c, dst, okeys):
                copy_col(carry[:, 4 + c:5 + c], dst[:, 0:1], okeys, ["carry"])
            return lru_dir(1, slot, [(0, T, init)], None, None, post, tmp_out=True, xset=xs)

        def fwd_own(t, slot, xs):
            init = h0col(0) if t == 0 else (lambda c: A[:, c, 2 + T - 1:2 + T])
            return lru_dir(0, slot, [(0, T, init)], lambda c: A[:, c, acols(t)], lambda c: [("A", c, t)], xset=xs)

        def bwd_own(t, slot, xs):
            init = carrycol(1) if t == 1 else (lambda c: A[:, c, 2 + 3 * T:2 + 3 * T + 1])
            return lru_dir(1, slot, [(0, T, init)], lambda c: A[:, c, acols(2 + t)], lambda c: [("A", c, 2 + t)], xset=xs)

        yield from convg(3, XC, 0)
        yield from merge(bwd_other(3, XC, 0), convg(2, XC + 4, 1))
        yield from merge(bwd_other(2, XC + 4, 1), convg(0, XC, 0))
        yield from merge(fwd_own(0, XC, 0), convg(1, XC + 4, 1))
        yield from fwd_own(1, XC + 4, 1)
        yield from bwd_own(1, XC + 4, 1)
        if not last:
            yield from hs_add(1)
            return
        yield from bwd_own(0, XC, 0)
        yield from hs_add(0)

    def hs_add(t):
        for c in range(4):
            tt(A[:, c, acols(t)], A[:, c, acols(t)], A[:, c, acols(2 + t)], ALU.add,
               [("A", c, t), ("A", c, 2 + t)], [("A", c, t)])
        yield

    def sample_front_tail():
        init = lambda c: A[:, c, 2 + 3 * T:2 + 3 * T + 1]
        yield from lru_dir(1, XC, [(0, T, init)], lambda c: A[:, c, acols(2)], lambda c: [("A", c, 2)], xset=0)
        yield from hs_add(0)

    HX = 16
    XT2 = LR
    plan = [(3, XT, HX), (2, XT2, HX), (0, XT, HK), (1, XT2, HK + 8)]
    rbs = {}
    xslots = [wload(win_d[0], 2048), wload(win_d[1], 2048)]
    for i in range(2):
        t, xb, hs_ = plan[i]
        rbs[t] = xr_front(xs_d, t * T, xb, SQ + 4 * (i % 2), 4)
    for i in range(4):
        t, xb, hs_ = plan[i]
        bks = xr_back_a(xb, rbs[t], 0, hs_, xslots)
        if i + 2 < 4:
            t2, xb2, _ = plan[i + 2]
            rbs[t2] = xr_front(xs_d, t2 * T, xb2, SQ + 4 * (i % 2), 4)
        xr_back_b(bks, [t], [(0, T, 2 + t * T)])

    def _w2calc():
        for v in range(2):
            stt(w2[:, :, v], modT[:, 32:40, v], 1.0, pp[:, _off["norm2"]:_off["norm2"] + 8], ALU.add, ALU.mult,
                mod2keys(v) + ["pp"], [("w2", v)])

    def _mk_hook(i):
        def h():
            mod_part(*MOD2[i])
            if i == 3:
                _w2calc()
        return h
    mod_hooks = [_mk_hook(i) for i in range(4)]
    interleave(phase_b2(xp_d, 0, True, H2, 1, 256, yp_d, 0), sample_front(), {None: 1.25, "f": 0.0, "d": 6.0})

    def sample_back(t):
        yield from phase_b1(HK + 8 * t, lambda m: A[:, m, acols(t)], lambda m: [("A", m, t)], finish=False)
        yield from phase_b2(xs_d, t * T, True, HK + 8 * t, 0, 64, ys_d, t * T, after_v=b1_finish)

    interleave(sample_back(1), sample_front_tail(), {None: 0.0, "f": 0.0, "d": 6.0})
    drain(sample_back(0))

    if dry:
        es.close()
        return wreq
    fkeys = ["store", "store2"]
    P.finalize(nc, fkeys)
    sems = {}
    for i, k in enumerate(P.semkeys):
        sems[k] = es.enter_context(nc.semaphore("s%d" % i))
    block = es.enter_context(nc.Block())
    P.emit_all(nc, block, sems, fkeys)
    es.close()
    return nc


def _fm(vec):
    return np.ascontiguousarray(np.asarray(vec, np.float32).reshape(-1, 128).T)


def _tile8(W, ncol=256):
    K, N = W.shape
    kc = K // 128
    return np.ascontiguousarray(W.reshape(kc, 128, N // ncol, ncol).transpose(2, 1, 0, 3).reshape(N // ncol, 128, kc * ncol))


_NC_CACHE = {}


def kernel(x_prompt, x_sample, state_lru, c, c_ctx, norm1, norm2, w_ada, b_ada, w_in,
           lru_conv_w, lru_conv_b, lru_wa, lru_ba, lru_wx, lru_bx, lru_lam, sgu_ws, sgu_bs,
           g_lru, g_sgu, w_out, ffn_up, ffn_conv_w, ffn_conv_b, ffn_down, final_norm):
    f = lambda a: np.asarray(a, np.float32)
    x_prompt, x_sample, state_lru, c, c_ctx = map(f, (x_prompt, x_sample, state_lru, c, c_ctx))
    w_ada, w_in, w_out, ffn_up, ffn_down = map(f, (w_ada, w_in, w_out, ffn_up, ffn_down))
    lru_wa, lru_wx, sgu_ws, sgu_bs = map(f, (lru_wa, lru_wx, sgu_ws, sgu_bs))
    wada_t = _tile8(w_ada[0])
    win_t = _tile8(w_in[0])
    wout_t = _tile8(w_out[0])
    up = ffn_up[0]
    upp = np.concatenate([up[:, :3072].reshape(1024, 24, 128), up[:, 3072:].reshape(1024, 24, 128)], axis=2).reshape(1024, 6144)
    wup_t = _tile8(upp)
    wdn_t = np.ascontiguousarray(ffn_down[0].reshape(2, 12, 128, 8, 128).transpose(3, 0, 2, 1, 4).reshape(16, 128, 1536))
    ident = np.eye(128, dtype=np.float32)
    in_maps = []
    for i in range(8):
        b, rev = i // 2, i % 2
        d0, d1 = (1, 0) if rev else (0, 1)
        xs = x_sample[b][::-1] if rev else x_sample[b]
        xp = x_prompt[2 * i:2 * i + 2]
        if rev:
            xp = xp[:, ::-1]
        pp = np.zeros((128, NPP), np.float32)

        def put(name, arr):
            pp[:, _off[name]:_off[name] + arr.shape[1]] = arr
        put("norm1", _fm(norm1[0])); put("norm2", _fm(norm2[0])); put("fnorm", _fm(final_norm))
        put("bada", _fm(b_ada[0]))
        cw = f(lru_conv_w[0])
        z = np.zeros((1, 512), np.float32)
        w5 = np.concatenate([z, cw[::-1]], 0) if rev else np.concatenate([cw, z], 0)
        put("cw", np.concatenate([_fm(w5[t_]) for t_ in range(5)], 1))
        put("cb", _fm(lru_conv_b[0]))
        put("ba", np.concatenate([_fm(f(lru_ba)[0, d0]), _fm(f(lru_ba)[0, d1])], 1))
        put("bx", np.concatenate([_fm(f(lru_bx)[0, d0]), _fm(f(lru_bx)[0, d1])], 1))
        put("lam", np.concatenate([_fm(f(lru_lam)[0, d0]), _fm(f(lru_lam)[0, d1])], 1))
        put("h0", np.concatenate([_fm(state_lru[b, 0, d0]), _fm(state_lru[b, 0, d1])], 1))
        put("glru", _fm(g_lru[0])); put("gsgu", _fm(g_sgu[0]))
        fcw = f(ffn_conv_w[0])
        if rev:
            fcw = fcw[::-1]
        put("fw", np.concatenate([_fm(fcw[t_]) for t_ in range(3)], 1))
        put("fb", _fm(ffn_conv_b[0]))
        pp[:, _off["eps"]] = EPS
        pp[:, _off["one"]] = 1.0
        pp[:, _off["oneg"]] = np.float32(1.0) + np.float32(4e-7)
        cvec = np.zeros((128, 16), np.float32)
        cvec[:, 0::2] = _fm(c[b])
        cvec[:, 1::2] = _fm(c_ctx)
        gwm = np.zeros((128, 16, 128), np.float32)
        for dpi, do in enumerate((d0, d1)):
            for kind, wsrc in enumerate((lru_wa, lru_wx)):
                for cc in range(4):
                    idx = dpi * 8 + kind * 4 + cc
                    gwm[0:64, idx, 0:64] = wsrc[0, do, 2 * cc]
                    gwm[64:128, idx, 64:128] = wsrc[0, do, 2 * cc + 1]
        ws = sgu_ws[0]
        bs = sgu_bs[0]
        if rev:
            ws = ws[:, ::-1, ::-1]
            bs = bs[:, ::-1]
        wsT = np.ascontiguousarray(ws.transpose(2, 0, 1)).reshape(128, 512)
        in_maps.append({
            "xs": np.ascontiguousarray(xs.T), "xp": np.ascontiguousarray(xp.reshape(512, 1024).T),
            "pp": pp, "cvec": cvec, "bsrow": np.ascontiguousarray(bs.reshape(1, 512)),
            "gw": gwm.reshape(128, 2048), "wsT": wsT,
            "wada": wada_t, "win": win_t, "wout": wout_t, "wup": wup_t, "wdn": wdn_t,
        })
    if "nc" not in _NC_CACHE:
        _NC_CACHE["nc"] = build_program()
    nc = _NC_CACHE["nc"]
    res = run_bass_kernel_spmd(nc, in_maps, core_ids=list(range(8)))
    y_prompt = np.zeros((16, 256, 1024), np.float32)
    y_sample = np.zeros((4, 2048, 1024), np.float32)
    new_state = np.zeros((16, 1, 2, 512), np.float32)
    for i in range(8):
        b, rev = i // 2, i % 2
        r = res.results[i]
        ys = np.asarray(r["ys"]).T
        yp = np.asarray(r["yp"]).T.reshape(2, 256, 1024)
        nstv = np.asarray(r["nst"]).reshape(128, 2, 2, 4).transpose(1, 2, 3, 0).reshape(2, 2, 512)
        if rev:
            y_sample[b, 1024:] = ys[::-1]
            y_prompt[2 * i:2 * i + 2] = yp[:, ::-1]
            new_state[2 * i:2 * i + 2, 0, 1] = nstv[:, 0]
            new_state[2 * i:2 * i + 2, 0, 0] = nstv[:, 1]
        else:
            y_sample[b, :1024] = ys
            y_prompt[2 * i:2 * i + 2] = yp
            new_state[2 * i:2 * i + 2, 0, 0] = nstv[:, 0]
            new_state[2 * i:2 * i + 2, 0, 1] = nstv[:, 1]
    return (y_prompt, y_sample, new_state)
```
